# Optimizing a Trainium2 kernel written in Bass

```python
import jax, jax.numpy as jnp
from jax import lax
import numpy as np

D_MODEL = 2048
BATCH = 1
SEQ = 8192
DEPTH = 4

GRID_W = 64
CTX_LEN = 256
N_HEADS = 8
N_KV_HEADS = 2
GROUP = N_HEADS // N_KV_HEADS
D_HEAD = 128
ATTN_W = N_HEADS * D_HEAD
KV_W = N_KV_HEADS * D_HEAD
WINDOW = 128
BLOCK = 128
FOURIER_W = D_MODEL // 4
N_FOURIER_GROUPS = 4
FOURIER_GROUP_W = FOURIER_W // N_FOURIER_GROUPS
POOL_WINDOWS = (2, 4, 8, 16)
POOL_W = D_MODEL // 4
POOL_GROUP_W = POOL_W // len(POOL_WINDOWS)
POOL_OUT_GROUP_W = D_MODEL // len(POOL_WINDOWS)
N_BRANCHES = 3
IN_W = ATTN_W + 2 * KV_W + FOURIER_W + POOL_W + N_BRANCHES * D_MODEL
D_FF = 5632
N_MOD = 9
ROPE_BASE = 10000.0
EPS = 1e-6
NEG_INF = -1e30

kernel_name = "hybrid_gated_dit_block"


def rms_norm(x, gain):
    xf = x.astype(jnp.float32)
    y = xf * lax.rsqrt(jnp.mean(xf * xf, axis=-1, keepdims=True) + EPS)
    return (y * gain.astype(jnp.float32)).astype(x.dtype)


def modulate(x, gain, shift, scale):
    return rms_norm(x, gain) * (1 + scale) + shift


def swiglu(h, wg, wu, wd):
    return (jax.nn.silu(h @ wg) * (h @ wu)) @ wd


def ffn_sublayer(s, gain, shift, scale, gate, wg, wu, wd):
    return s + 0.5 * gate * swiglu(modulate(s, gain, shift, scale), wg, wu, wd)


def axial_rope_tables(S):
    rows_n = S // GRID_W
    rows = jnp.repeat(jnp.arange(rows_n), GRID_W).astype(jnp.float32)
    cols = jnp.tile(jnp.arange(GRID_W), rows_n).astype(jnp.float32)
    n_freq = D_HEAD // 4
    freqs = ROPE_BASE ** (-jnp.arange(n_freq, dtype=jnp.float32) / n_freq)
    ang = jnp.stack([rows[:, None] * freqs, cols[:, None] * freqs], axis=1)
    return jnp.cos(ang), jnp.sin(ang)


def apply_rope(x, cos, sin):
    B, S, H, _ = x.shape
    xr = x.astype(jnp.float32).reshape(B, S, H, 2, 2, D_HEAD // 4)
    x1, x2 = xr[..., 0, :], xr[..., 1, :]
    c = cos[None, :, None]
    s = sin[None, :, None]
    out = jnp.stack([x1 * c - x2 * s, x2 * c + x1 * s], axis=-2)
    return out.reshape(B, S, H, D_HEAD).astype(x.dtype)


def kv_heads(k, v, k_gain):
    B, S = k.shape[:2]
    k = rms_norm(k.reshape(B, S, N_KV_HEADS, D_HEAD), k_gain)
    return k, v.reshape(B, S, N_KV_HEADS, D_HEAD)


def project(h, w_in, q_gain, k_gain):
    B, S = h.shape[:2]
    z = h @ w_in
    o1 = ATTN_W
    o2 = o1 + KV_W
    o3 = o2 + KV_W
    o4 = o3 + FOURIER_W
    o5 = o4 + POOL_W
    q, k, v, uf, up, g = jnp.split(z, [o1, o2, o3, o4, o5], axis=-1)
    q = rms_norm(q.reshape(B, S, N_HEADS, D_HEAD), q_gain)
    k, v = kv_heads(k, v, k_gain)
    return q, k, v, uf, up, g


def sink_softmax(logits, sink):
    sink_col = jnp.broadcast_to(sink.astype(jnp.float32).reshape(N_KV_HEADS, GROUP, 1, 1),
                                logits.shape[:-1] + (1,))
    p = jax.nn.softmax(jnp.concatenate([logits, sink_col], axis=-1), axis=-1)
    return p[..., :-1]


def window_attention(q, k, v, kc, vc, sink):
    B, S = q.shape[:2]
    nb = S // BLOCK
    scale = D_HEAD ** -0.5
    qb = q.reshape(B, nb, BLOCK, N_KV_HEADS, GROUP, D_HEAD)

    def bands(t):
        tp = jnp.pad(t, ((0, 0), (BLOCK, BLOCK), (0, 0), (0, 0)))
        return jnp.concatenate(
            [tp[:, i * BLOCK:i * BLOCK + S].reshape(B, nb, BLOCK, N_KV_HEADS, D_HEAD) for i in range(3)],
            axis=2)

    kb, vb = bands(k), bands(v)
    s_loc = jnp.einsum('bnqkgd,bnskd->bnkgqs', qb, kb).astype(jnp.float32) * scale
    s_ctx = jnp.einsum('bnqkgd,bckd->bnkgqc', qb, kc).astype(jnp.float32) * scale
    qi = jnp.arange(BLOCK)[:, None]
    kj = jnp.arange(3 * BLOCK)[None, :]
    key_pos = jnp.arange(nb)[:, None, None] * BLOCK - BLOCK + kj[None]
    mask = (jnp.abs(kj - BLOCK - qi) <= WINDOW)[None] & (key_pos >= 0) & (key_pos < S)
    s_loc = jnp.where(mask[None, :, None, None], s_loc, NEG_INF)
    p = sink_softmax(jnp.concatenate([s_loc, s_ctx], axis=-1), sink).astype(v.dtype)
    p_loc, p_ctx = p[..., :3 * BLOCK], p[..., 3 * BLOCK:]
    out = (jnp.einsum('bnkgqs,bnskd->bnqkgd', p_loc, vb)
           + jnp.einsum('bnkgqc,bckd->bnqkgd', p_ctx, vc))
    return out.reshape(B, S, ATTN_W)


def context_attention(qc, kc, vc, sink):
    B, C = qc.shape[:2]
    scale = D_HEAD ** -0.5
    qg = qc.reshape(B, C, N_KV_HEADS, GROUP, D_HEAD)
    s = jnp.einsum('bqkgd,bskd->bkgqs', qg, kc).astype(jnp.float32) * scale
    p = sink_softmax(s, sink).astype(vc.dtype)
    return jnp.einsum('bkgqs,bskd->bqkgd', p, vc).reshape(B, C, ATTN_W)


def fourier_mix(uf):
    B, S = uf.shape[:2]
    u = uf.astype(jnp.float32).reshape(B, S, N_FOURIER_GROUPS, FOURIER_GROUP_W)
    y = jnp.real(jnp.fft.fft2(u, axes=(1, 3), norm='ortho'))
    return y.reshape(B, S, FOURIER_W).astype(uf.dtype)


def pool_mix(up, w_pool, pool_scale):
    B, S = up.shape[:2]
    u = up.astype(jnp.float32).reshape(B, S, len(POOL_WINDOWS), POOL_GROUP_W)
    cs = jnp.pad(jnp.cumsum(u, axis=1), ((0, 0), (1, 0), (0, 0), (0, 0)))
    t = jnp.arange(S)
    outs = []
    for gi, w in enumerate(POOL_WINDOWS):
        lo = jnp.clip(t - w // 2, 0, S)
        hi = jnp.clip(t + (w - w // 2), 0, S)
        cs_g = cs[:, :, gi]
        mean = (cs_g[:, hi] - cs_g[:, lo]) / (hi - lo).astype(jnp.float32)[None, :, None]
        outs.append(mean - u[:, :, gi])
    pooled = jnp.stack(outs, axis=2).astype(up.dtype)
    y = jnp.einsum('bsgc,gcd->bsgd', pooled, w_pool).reshape(B, S, D_MODEL)
    return y * pool_scale


def merge(a, uf, up, g, w_attn_o, w_fourier, w_pool, pool_scale, w_out):
    ga, gf, gp = jnp.split(jax.nn.sigmoid(g), N_BRANCHES, axis=-1)
    y = (ga * (a @ w_attn_o)
         + gf * (fourier_mix(uf) @ w_fourier)
         + gp * pool_mix(up, w_pool, pool_scale))
    return y @ w_out


def setup_inputs(seed: int = 0) -> dict:
    key = jax.random.key(seed)
    ks = jax.random.split(key, 19)
    f32 = jnp.float32

    def nrm(k, shape, scale):
        return jax.random.normal(k, shape, f32) * scale

    return {
        "x": nrm(ks[0], (BATCH, SEQ, D_MODEL), 1.0),
        "c": nrm(ks[1], (BATCH, D_MODEL), 1.0),
        "ctx": nrm(ks[2], (BATCH, CTX_LEN, D_MODEL), 1.0),
        "c_ctx": nrm(ks[3], (D_MODEL,), 1.0),
        "w_ada": nrm(ks[4], (DEPTH, D_MODEL, N_MOD * D_MODEL), 0.5 * D_MODEL ** -0.5),
        "b_ada": nrm(ks[5], (DEPTH, N_MOD * D_MODEL), 0.01),
        "norm_w": 1.0 + nrm(ks[6], (DEPTH, 3, D_MODEL), 0.05),
        "ffn_w_gate": nrm(ks[7], (DEPTH, 2, D_MODEL, D_FF), D_MODEL ** -0.5),
        "ffn_w_up": nrm(ks[8], (DEPTH, 2, D_MODEL, D_FF), D_MODEL ** -0.5),
        "ffn_w_down": nrm(ks[9], (DEPTH, 2, D_FF, D_MODEL), D_FF ** -0.5),
        "w_in": nrm(ks[10], (DEPTH, D_MODEL, IN_W), D_MODEL ** -0.5),
        "q_gain": 1.0 + nrm(ks[11], (DEPTH, D_HEAD), 0.05),
        "k_gain": 1.0 + nrm(ks[12], (DEPTH, D_HEAD), 0.05),
        "sink": nrm(ks[13], (DEPTH, N_HEADS), 0.5),
        "w_attn_o": nrm(ks[14], (DEPTH, ATTN_W, D_MODEL), ATTN_W ** -0.5),
        "w_fourier": nrm(ks[15], (DEPTH, FOURIER_W, D_MODEL), FOURIER_W ** -0.5),
        "w_pool": nrm(ks[16], (DEPTH, len(POOL_WINDOWS), POOL_GROUP_W, POOL_OUT_GROUP_W), POOL_GROUP_W ** -0.5),
        "pool_scale": 1.0 + nrm(ks[17], (DEPTH, D_MODEL), 0.05),
        "w_out": nrm(ks[18], (DEPTH, D_MODEL, D_MODEL), D_MODEL ** -0.5),
    }


def reference(x, c, ctx, c_ctx, w_ada, b_ada, norm_w, ffn_w_gate, ffn_w_up, ffn_w_down,
              w_in, q_gain, k_gain, sink, w_attn_o, w_fourier, w_pool, pool_scale, w_out):
    B, S, _ = x.shape
    cos, sin = axial_rope_tables(S)
    cond_x = jax.nn.silu(c)
    cond_c = jax.nn.silu(c_ctx)
    for l in range(DEPTH):
        last = l == DEPTH - 1
        mod_x = (cond_x @ w_ada[l] + b_ada[l]).reshape(B, N_MOD, 1, D_MODEL)
        mod_c = (cond_c @ w_ada[l] + b_ada[l]).reshape(N_MOD, 1, D_MODEL)
        mx = [mod_x[:, i] for i in range(N_MOD)]
        mc = [mod_c[i] for i in range(N_MOD)]

        x = ffn_sublayer(x, norm_w[l, 0], mx[0], mx[1], mx[2],
                         ffn_w_gate[l, 0], ffn_w_up[l, 0], ffn_w_down[l, 0])
        ctx = ffn_sublayer(ctx, norm_w[l, 0], mc[0], mc[1], mc[2],
                           ffn_w_gate[l, 0], ffn_w_up[l, 0], ffn_w_down[l, 0])

        hx = modulate(x, norm_w[l, 1], mx[3], mx[4])
        hc = modulate(ctx, norm_w[l, 1], mc[3], mc[4])
        qx, kx, vx, ufx, upx, gx = project(hx, w_in[l], q_gain[l], k_gain[l])
        qx = apply_rope(qx, cos, sin)
        kx = apply_rope(kx, cos, sin)
        if last:
            zkv = hc @ w_in[l][:, ATTN_W:ATTN_W + 2 * KV_W]
            kc, vc = kv_heads(zkv[..., :KV_W], zkv[..., KV_W:], k_gain[l])
        else:
            qc, kc, vc, ufc, upc, gc = project(hc, w_in[l], q_gain[l], k_gain[l])
        ax = window_attention(qx, kx, vx, kc, vc, sink[l])
        mix_x = merge(ax, ufx, upx, gx, w_attn_o[l], w_fourier[l], w_pool[l], pool_scale[l], w_out[l])
        x = x + mx[5] * mix_x
        if not last:
            ac = context_attention(qc, kc, vc, sink[l])
            mix_c = merge(ac, ufc, upc, gc, w_attn_o[l], w_fourier[l], w_pool[l], pool_scale[l], w_out[l])
            ctx = ctx + mc[5] * mix_c

        x = ffn_sublayer(x, norm_w[l, 2], mx[6], mx[7], mx[8],
                         ffn_w_gate[l, 1], ffn_w_up[l, 1], ffn_w_down[l, 1])
        if not last:
            ctx = ffn_sublayer(ctx, norm_w[l, 2], mc[6], mc[7], mc[8],
                               ffn_w_gate[l, 1], ffn_w_up[l, 1], ffn_w_down[l, 1])
    return x
```

```python
import numpy as np
import ml_dtypes
import concourse.bass as bass
import concourse.mybir as mybir
from concourse.bass_utils import run_bass_kernel_spmd

F32 = mybir.dt.float32
BF16 = mybir.dt.bfloat16
AF = mybir.ActivationFunctionType
ALU = mybir.AluOpType

CFG = dict(DEPTH=4, SEQ=8192, D_FF=5632)
D = 2048
KC = 16
CTX = 256
NT = 512
EPS = 1e-6
GRID_W = 64


class Res:
    def __init__(self, name, ap=None, parent=None):
        self.name = name
        self.ap = ap
        self.last_w = None
        self.readers = []
        self.parent = parent
        self.children = []
        self.dma_sem = None
        self.dma_cnt = 0
        if parent is not None:
            parent.children.append(self)


class Sched:
    COMPUTE = ("pe", "act", "dve", "pool")
    ALL = ("pe", "act", "dve", "pool", "sp")

    def __init__(self, nc, es):
        self.nc = nc
        self.es = es
        self.streams = {e: [] for e in self.ALL}
        self.esem = {e: es.enter_context(nc.semaphore("S_" + e)) for e in self.COMPUTE}
        self.tick = {e: 0 for e in self.COMPUTE}
        self.seen = {e: {} for e in self.ALL}
        self.pending_stores = {}
        self.nsem = 0
        self.pid = {}

    def _conflicts(self, r, write):
        toks = []
        def add(res):
            if res.last_w is not None:
                toks.append(res.last_w)
            if write:
                toks.extend(res.readers)
        add(r)
        if r.parent is not None:
            add(r.parent)
        for c in r.children:
            add(c)
        return toks

    def _waits(self, eng, reads, writes):
        need = {}
        for r in reads:
            for t in self._conflicts(r, False):
                self._need(eng, need, t)
        for r in writes:
            for t in self._conflicts(r, True):
                self._need(eng, need, t)
        out = []
        for key, (sem, val) in need.items():
            if self.seen[eng].get(key, 0) < val:
                self.seen[eng][key] = val
                out.append((sem, val))
        return out

    def _need(self, eng, need, tok):
        kind, key, sem, val = tok
        if kind == "eng" and key == eng:
            return
        if kind == "eng":
            assert val <= self.tick[key], "dangling PE mark dependency (would deadlock)"
        k = (kind, key)
        if k not in need or need[k][1] < val:
            need[k] = (sem, val)

    def _record(self, tok, reads, writes):
        for r in reads:
            r.readers.append(tok)
        for r in writes:
            r.last_w = tok
            r.readers = []

    def op(self, eng, fn, reads=(), writes=(), mark=True):
        waits = self._waits(eng, reads, writes)
        if mark:
            self.tick[eng] += 1
            tok = ("eng", eng, self.esem[eng], self.tick[eng])
            inc = (self.esem[eng], 1)
        else:
            tok = ("eng", eng, self.esem[eng], self.tick[eng] + 1)
            inc = None
        self.streams[eng].append((waits, fn, inc))
        self._record(tok, reads, writes)

    def dma(self, q, out, in_, sb, reads=(), writes=(), store=False, **kw):
        if sb.dma_sem is None:
            sb.dma_sem = self.es.enter_context(self.nc.semaphore("D_%d" % self.nsem))
            self.nsem += 1
        waits = self._waits(q, reads, writes)
        sb.dma_cnt += 16
        tok = ("dma", sb.name, sb.dma_sem, sb.dma_cnt)
        def _f(e, out=out, in_=in_, kw=kw):
            pid = self.pid[id(e)]
            o = out(pid) if callable(out) else out
            i = in_(pid) if callable(in_) else in_
            try:
                return e.dma_start(out=o, in_=i, **kw)
            except Exception:
                print("DMA FAIL", o, i, kw)
                raise
        self.streams[q].append((waits, _f, (sb.dma_sem, 16)))
        self._record(tok, reads, writes)
        if store:
            self.pending_stores[sb.name] = tok
        return tok

    def barrier(self):
        for q in ("sp", "pool"):
            waits = []
            for tok in self.pending_stores.values():
                kind, key, sem, val = tok
                k = (kind, key)
                if self.seen[q].get(k, 0) < val:
                    self.seen[q][k] = val
                    waits.append((sem, val))
            if waits:
                self.streams[q].append((waits, None, None))
        self.pending_stores = {}

    def custom(self, q, fn, waits=()):
        self.streams[q].append((list(waits), fn, None))

    def pending_waits(self, q):
        waits = []
        for tok in self.pending_stores.values():
            kind, key, sem, val = tok
            k = (kind, key)
            if self.seen[q].get(k, 0) < val:
                self.seen[q][k] = val
                waits.append((sem, val))
        return waits

    def final_wait(self, q, toks):
        waits = [(t[2], t[3]) for t in toks]
        self.streams[q].append((waits, None, None))

    def emit(self, block):
        nc = self.nc
        def run(stream, need_pid=False):
            def f(e):
                self.pid[id(e)] = e.partition_id() if need_pid else None
                for waits, fn, inc in stream:
                    for sem, val in waits:
                        e.wait_ge(sem, val)
                    if fn is not None:
                        ins = fn(e)
                        if inc is not None:
                            ins.then_inc(*inc)
            return f
        block.tensor(run(self.streams["pe"]))
        block.scalar(run(self.streams["act"], True))
        block.vector(run(self.streams["dve"]))
        block.gpsimd(run(self.streams["pool"], True))
        block.sync(run(self.streams["sp"], True))


def _fm_tiles(w, nchunk):
    K = w.shape[0]
    return np.ascontiguousarray(w.reshape(K // 128, 128, nchunk, 128).transpose(2, 1, 0, 3))


NPIECE = 136
NCORES = 2
PADT = 128


def _wlayout(NJ):
    items = [("gu0", 4096, NJ), ("dn0", NJ * 128, 16), ("gu1", 4096, NJ), ("dn1", NJ * 128, 16),
             ("wqk", 2048, 10), ("wv", 4096, 1), ("wfp", 2048, 8), ("wg", 2048, 48),
             ("wmg", 13 * 128, 16), ("wout", 2048, 16)]
    lay = {}
    off = 0
    for name, f, nt in items:
        lay[name] = (off, f, nt)
        off += 128 * f * nt
    q = 2048 * NPIECE * NCORES * 16
    tot = (off + q - 1) // q * q
    return lay, tot


def _constants(SEQ):
    TA = SEQ + CTX
    rows_n = SEQ // GRID_W
    rows = np.repeat(np.arange(rows_n), GRID_W).astype(np.float32)
    cols = np.tile(np.arange(GRID_W), rows_n).astype(np.float32)
    n_freq = 32
    freqs = (10000.0 ** (-np.arange(n_freq, dtype=np.float32) / n_freq)).astype(np.float32)
    ar = rows[None, :] * freqs[:, None]
    ac = cols[None, :] * freqs[:, None]
    cosT = np.concatenate([np.cos(ar), np.cos(ar), np.cos(ac), np.cos(ac)], 0).astype(np.float32)
    sinT = np.concatenate([-np.sin(ar), np.sin(ar), -np.sin(ac), np.sin(ac)], 0).astype(np.float32)
    rope = np.ascontiguousarray(np.stack([cosT, sinT], 1))
    perm = np.zeros((128, 128), np.float32)
    for d in range(128):
        src = d + 32 if (d % 64) < 32 else d - 32
        perm[src, d] = 1.0
    ident = np.eye(128, dtype=np.float32)
    ones = np.ones((128, 128), np.float32)
    kk = np.arange(128)[:, None]
    qq = np.arange(128)[None, :]
    mleft = np.where(kk >= qq, 0.0, -30000.0).astype(np.float32)
    mright = np.where(kk <= qq, 0.0, -30000.0).astype(np.float32)
    mats = np.stack([perm, ident, ones, np.tile(mleft, (1, 4))[:, :128] * 0], 0)
    small = np.concatenate([perm, ident, ones], 1).astype(ml_dtypes.bfloat16)
    masks = np.concatenate([np.tile(mleft, (1, 4)), np.tile(mright, (1, 4))], 1).astype(ml_dtypes.bfloat16)
    cc = np.arange(128)
    angc = 2 * np.pi * np.outer(cc, cc) / 128.0
    csc = (np.concatenate([np.cos(angc), np.sin(angc)], 1) / np.sqrt(128.0)).astype(ml_dtypes.bfloat16)
    s = np.arange(SEQ, dtype=np.int64)
    idx = np.outer(s, s) % SEQ
    ang = (2 * np.pi / SEQ) * idx
    tab = np.empty((2, SEQ, SEQ), ml_dtypes.bfloat16)
    tab[0] = (np.cos(ang) / np.sqrt(SEQ)).astype(ml_dtypes.bfloat16)
    tab[1] = (-np.sin(ang) / np.sqrt(SEQ)).astype(ml_dtypes.bfloat16)
    del ang, idx
    rc = np.zeros((4, TA), np.float32)
    for gi, w in enumerate((2, 4, 8, 16)):
        for (base, n) in ((0, SEQ), (SEQ, CTX)):
            t = np.arange(n)
            lo = np.clip(t - w // 2, 0, n)
            hi = np.clip(t + (w - w // 2), 0, n)
            rc[gi, base:base + n] = 1.0 / (hi - lo)
    rcb = np.ascontiguousarray(np.broadcast_to(rc[None], (128, 4, TA)))
    return dict(rope=rope, small=small, masks=masks, csc=csc, tab=tab, rc=rcb)


def _prep_inputs(inp, cfg):
    L, SEQ, DFF = cfg["DEPTH"], cfg["SEQ"], cfg["D_FF"]
    NJ = DFF // 128
    f = lambda a: np.asarray(a, dtype=np.float32)
    x = f(inp["x"])[0]
    ctx = f(inp["ctx"])[0]
    xa = np.concatenate([x, ctx], 0)
    m = {}
    m["xin"] = np.ascontiguousarray(xa.T.reshape(KC, 128, SEQ + CTX))
    cc = np.stack([f(inp["c"])[0], f(inp["c_ctx"])], -1)
    m["cin"] = np.ascontiguousarray(cc.reshape(KC, 128, 2).transpose(1, 0, 2))
    w_ada = f(inp["w_ada"])
    m["wada"] = np.stack([_fm_tiles(w_ada[l], 144) for l in range(L)], 0)
    m["bada"] = np.ascontiguousarray(f(inp["b_ada"])[:L].reshape(L, 144, 128).transpose(2, 0, 1))
    m["normw"] = np.ascontiguousarray(f(inp["norm_w"])[:L].reshape(L, 3, KC, 128).transpose(3, 0, 1, 2))
    g = f(inp["ffn_w_gate"]); u = f(inp["ffn_w_up"]); dn = f(inp["ffn_w_down"])
    gu = np.empty((L, 2, NJ, 128, 2, KC, 128), np.float32)
    dd = np.empty((L, 2, KC, 128, NJ, 128), np.float32)
    for l in range(L):
        for s in range(2):
            gu[l, s, :, :, 0] = _fm_tiles(g[l, s], NJ)
            gu[l, s, :, :, 1] = _fm_tiles(u[l, s], NJ)
            dd[l, s] = dn[l, s].reshape(NJ, 128, KC, 128).transpose(2, 1, 0, 3)
    m["wgu"] = gu
    m["wdn"] = dd
    w_in = f(inp["w_in"])
    m["wqk"] = np.stack([_fm_tiles(w_in[l][:, 0:1280], 10) for l in range(L)], 0)
    m["wv"] = np.stack([np.ascontiguousarray(w_in[l][:, 1280:1536].reshape(KC, 128, 256).transpose(1, 0, 2)) for l in range(L)], 0)
    m["wfp"] = np.stack([_fm_tiles(w_in[l][:, 1536:2560], 8) for l in range(L)], 0)
    m["wg"] = np.stack([_fm_tiles(w_in[l][:, 2560:8704], 48) for l in range(L)], 0)
    wao = f(inp["w_attn_o"]); wfo = f(inp["w_fourier"]); wpl = f(inp["w_pool"])
    mm = np.empty((L, KC, 128, 13, 128), np.float32)
    for l in range(L):
        mm[l, :, :, 0:8] = _fm_tiles(wao[l], 16)
        mm[l, :, :, 8:12] = _fm_tiles(wfo[l], 16)
        mm[l, :, :, 12] = wpl[l].reshape(4, 128, 4, 128).transpose(0, 2, 1, 3).reshape(16, 128, 128)
    m["wmg"] = mm
    m["wout"] = np.stack([_fm_tiles(f(inp["w_out"])[l], 16) for l in range(L)], 0)
    qg = f(inp["q_gain"])[:L]; kg = f(inp["k_gain"])[:L]
    m["qkg"] = np.ascontiguousarray(np.stack([qg, kg], -1).transpose(1, 0, 2))
    m["sink"] = np.ascontiguousarray(np.broadcast_to(f(inp["sink"])[:L][None], (128, L, 8)))
    m["pscale"] = np.ascontiguousarray(f(inp["pool_scale"])[:L].reshape(L, KC, 128).transpose(2, 0, 1))
    lay, tot = _wlayout(NJ)
    blob = np.zeros((L, tot), np.float32)
    for l in range(L):
        parts = [gu[l, 0], dd[l, 0], gu[l, 1], dd[l, 1], m["wqk"][l], m["wv"][l], m["wfp"][l], m["wg"][l],
                 m["wmg"][l], m["wout"][l]]
        o = 0
        for p_ in parts:
            blob[l, o:o + p_.size] = p_.ravel()
            o += p_.size
    for k in ("wgu", "wdn", "wqk", "wv", "wfp", "wg", "wmg", "wout"):
        del m[k]
    m.update(_constants(SEQ))
    HALF = SEQ // NCORES
    HW = tot // NCORES
    kk = np.arange(128)[:, None]
    qq = np.arange(128)[None, :]
    mleft = np.tile(np.where(kk >= qq, 0.0, -30000.0), (1, 4)).astype(np.float32)
    mright = np.tile(np.where(kk <= qq, 0.0, -30000.0), (1, 4)).astype(np.float32)
    dead = np.full((128, 512), -30000.0, np.float32)
    nonce = np.array([[int(np.random.randint(1, 2 ** 30)), 0, 0, 0]], np.int32)
    xin = m.pop("xin"); wada = m.pop("wada"); bada = m.pop("bada")
    maps = []
    for c in range(NCORES):
        mc = dict(m)
        mc["xin"] = np.ascontiguousarray(np.concatenate([xin[:, :, c * HALF:(c + 1) * HALF], xin[:, :, SEQ:]], 2))
        mc["wada"] = np.ascontiguousarray(wada[:, c * 72:(c + 1) * 72])
        mc["bada"] = np.ascontiguousarray(bada[:, :, c * 72:(c + 1) * 72])
        mc["wblob"] = np.ascontiguousarray(blob[:, c * HW:(c + 1) * HW])
        eL = dead if c == 0 else mleft
        eR = dead if c == NCORES - 1 else mright
        mc["emask"] = np.concatenate([eL, eR], 1).astype(ml_dtypes.bfloat16)
        mc["eflag"] = np.ascontiguousarray(np.broadcast_to(
            np.array([[0.0 if c == 0 else 1.0, 0.0 if c == NCORES - 1 else 1.0]], np.float32), (128, 2)))
        mc["rope"] = np.ascontiguousarray(m["rope"][:, :, c * HALF:(c + 1) * HALF])
        mc["rc"] = np.ascontiguousarray(np.concatenate([m["rc"][:, :, c * HALF:(c + 1) * HALF], m["rc"][:, :, SEQ:]], 2))
        sg = (np.arange(SEQ) + c * HALF) % SEQ
        mc["tab"] = np.ascontiguousarray(m["tab"][:, sg, c * HALF:(c + 1) * HALF])
        mc["ctab"] = np.ascontiguousarray(m["tab"][:, ::SEQ // CTX, 0:CTX][:, 0:CTX])
        mc["nonce"] = nonce
        maps.append(mc)
    return maps


def build_program(cfg, shapes):
    from contextlib import ExitStack
    L, SEQ, DFF = cfg["DEPTH"], cfg["SEQ"], cfg["D_FF"]
    NJ = DFF // 128
    TA = SEQ + CTX
    NXG = SEQ // NT
    HALF = SEQ // NCORES
    NXGL = NXG // NCORES
    assert NXG % NCORES == 0
    TL = HALF + CTX
    TP = SEQ + CTX + 3 * PADT
    CTXP = SEQ + 2 * PADT
    groups = [(g * NT, NT, False) for g in range(NXGL)] + [(HALF, CTX, True)]
    NB = SEQ // 128

    def lsl(t0l, n, is_ctx):
        st = SEQ + (t0l - HALF) if is_ctx else t0l
        return slice(st, st + n)
    att_scale = 128.0 ** -0.5

    nc = bass.Bass("TRN2", target_bir_lowering=False)
    dt_of = {np.dtype(np.float32): F32, np.dtype(ml_dtypes.bfloat16): BF16, np.dtype(np.int32): mybir.dt.int32}
    IN = {k: nc.dram_tensor(k, list(shp), dt_of[np.dtype(dtp)], kind="ExternalInput").ap()
          for k, (shp, dtp) in shapes.items()}
    yout = nc.dram_tensor("yout", [KC, 128, HALF], F32, kind="ExternalOutput").ap()

    def dscr(name, shape, dt):
        return nc.dram_tensor(name, shape, dt, kind="Internal").ap()
    def dshr(name, shape, dt):
        return nc.dram_tensor(name, shape, dt, kind="Internal", addr_space="Shared").ap()
    XD = dscr("XD", [KC, 128, TL], F32)
    QD = [dscr("QD%d" % i, [8, 128, TL], BF16) for i in range(2)]
    GD = [dscr("GD%d" % i, [48, 128, TL], BF16) for i in range(2)]
    KD = [dscr("KD%d" % i, [2, 128, TA], BF16) for i in range(2)]
    VD = [dscr("VD%d" % i, [TA, 256], BF16) for i in range(2)]
    UD = [dscr("UD%d" % i, [TA, 1024], BF16) for i in range(2)]
    PD = [dscr("PD%d" % i, [4, 128, TA], F32) for i in range(2)]
    SHK = [dshr("SHK%d" % i, [NCORES, 2, 128, HALF], BF16) for i in range(2)]
    SHV = [dshr("SHV%d" % i, [NCORES, HALF, 256], BF16) for i in range(2)]
    SHU = [dshr("SHU%d" % i, [NCORES, HALF, 1024], BF16) for i in range(2)]
    SHP = [dshr("SHP%d" % i, [NCORES, 4, 128, HALF], F32) for i in range(2)]
    LAY, WTOT = _wlayout(NJ)
    HW = WTOT // NCORES
    WB2 = [dshr("WB%d" % i, [NCORES, HW // 2048, 2048], BF16) for i in range(L)]
    WB = [w.rearrange("n r c -> (n r c)") for w in WB2]
    MODS = dshr("MODS", [NCORES, 128, L * 144], F32)
    FLAGS = dshr("FLAGS", [NCORES, 16], mybir.dt.int32)
    DUMMY = dscr("DUMMY", [1, 4], mybir.dt.int32)

    es = ExitStack()
    with es:
        S = Sched(nc, es)
        def sb(name, shape, dt, parent=None):
            t = es.enter_context(nc.sbuf_tensor("s_" + name, shape, dt))
            return t, Res(name, parent=parent)
        def ps(name):
            t = es.enter_context(nc.psum_tensor(name, [128, 512], F32))
            return t, Res(name)

        PS = [ps("ps%d" % i) for i in range(8)]
        xg, xg_r = sb("xg", [128, KC, NT], F32)
        hb, hb_r = sb("hb", [128, KC, NT], BF16)
        NJA = max(NJ, 61)
        actb, act_r = sb("actb", [128, NJA * NT], BF16)
        sqb, sq_r = sb("sqb", [128, KC * NT], BF16)
        WSL = 3
        WBYTES = max(NJ * 128, 4096)
        wr = [sb("wr%d" % i, [128, WBYTES], BF16) for i in range(WSL)]
        wr_i = [0]
        modt, mod_r = sb("modt", [128, L * 9 * KC * 2], F32)
        amul, amul_r = sb("amul", [128, L * 3 * KC * 2], F32)
        hgt, hg_r = sb("hgt", [128, L * 2 * KC * 2], F32)
        normw, normw_r = sb("normw", [128, L * 3 * KC], F32)
        bada, bada_r = sb("bada", [128, L * 72], F32)
        modh, modh_r = sb("modh", [128, L * 144], F32)
        emask, emask_r = sb("emask", [128, 1024], BF16)
        eflag, eflag_r = sb("eflag", [128, 2], F32)
        nzt, nzt_r = sb("nzt", [1, 4], mybir.dt.int32)
        zt, zt_r = sb("zt", [128, 256], F32)
        cin, cin_r = sb("cin", [128, KC * 2], F32)
        cond, cond_r = sb("cond", [128, KC * 2], F32)
        qkg, qkg_r = sb("qkg", [128, L * 2], F32)
        sinkt, sink_r = sb("sinkt", [128, L * 8], F32)
        pscale, pscale_r = sb("pscale", [128, L * KC], F32)
        small, small_r = sb("small", [128, 384], BF16)
        masks, masks_r = sb("masks", [128, 1024], BF16)
        csc, csc_r = sb("csc", [128, 256], BF16)
        rope, rope_r = sb("rope", [128, 2 * NT], F32)
        rstd, rstd_r = sb("rstd", [128, NT], F32)
        tmpa, tmpa_r = sb("tmpa", [128, NT], F32)
        tmpb, tmpb_r = sb("tmpb", [128, NT], F32)
        tmpc, tmpc_r = sb("tmpc", [128, NT], F32)
        silu_t, silu_r = sb("silu_t", [128, NT], F32)
        stg = [sb("stg%d" % i, [128, NT], BF16) for i in range(3)]
        stg_i = [0]
        stf, stf_r = sb("stf", [128, NT], F32)

        perm_ap = small[:, 0:128]
        ident_ap = small[:, 128:256]
        ones_ap = small[:, 256:384]

        off = [0]
        def carve(name, n, dt=BF16):
            nb = n * (2 if dt == F32 else 1)
            assert off[0] + nb <= NJA * NT, "act scratch overflow"
            v = actb[:, off[0]:off[0] + nb]
            if dt == F32:
                v = v.bitcast(F32)
            off[0] += nb
            return v, Res(name, parent=act_r)
        qgt, qgt_r = carve("qgt", 8 * NT)
        kt, kt_r = carve("kt", 2 * 8 * 128)
        vt, vt_r = carve("vt", 8 * 256)
        rden, rden_r = carve("rden", NT, F32)
        at, at_r = carve("at", 8 * NT)
        yt, yt_r = carve("yt", 4 * NT)
        pt, pt_r = carve("pt", 4 * NT)
        upt, upt_r = carve("upt", 4 * (NT + 16), F32)
        pa, pa_r = carve("pa", NT + 16, F32)
        pb, pb_r = carve("pb", NT + 16, F32)
        rct, rct_r = carve("rct", 4 * NT, F32)
        gat = [carve("gat%d" % i, 3 * NT) for i in range(2)]
        soff = [0]
        def carve_s(name, n):
            assert soff[0] + n <= KC * NT
            v = sqb[:, soff[0]:soff[0] + n]
            soff[0] += n
            return v, Res(name, parent=sq_r)
        ucs = [carve_s("ucs%d" % i, 1024) for i in range(3)]
        tabt = [carve_s("tab%d" % i, 1024) for i in range(3)]
        ebuf = [carve_s("e%d" % i, NT) for i in range(2)]
        sexp, sexp_r = sb("sexp", [128, L * 8], F32)

        def next_w():
            i = wr_i[0] % WSL
            wr_i[0] += 1
            return wr[i]

        pc_res = [Res("pc%d" % l) for l in range(L)]
        PIECE = HW // 2

        def precast(l, i):
            a, b = i * PIECE, (i + 1) * PIECE
            S.dma("pool", lambda pid: WB2[l][bass.ds(pid, 1), a // 2048:a // 2048 + PIECE // 2048, :].rearrange("o r c -> (o r) c"),
                  IN["wblob"][l, a:b].rearrange("(r c) -> r c", c=2048),
                  pc_res[l], writes=[pc_res[l]], store=True, max_dma_last_dim=8192)

        def load_w(l, name, idx):
            base, f, nt = LAY[name]
            assert idx < nt
            off_ = base + idx * 128 * f
            t, r = next_w()
            S.dma("pool", t[:, 0:f], WB[l][off_:off_ + 128 * f].rearrange("(p f) -> p f", f=f), r,
                  reads=[pc_res[l]], writes=[r])
            lw_cnt[0] += 1
            if False:
                precast(*pc_queue.pop(0))
            return t, r

        pc_queue = []
        lw_cnt = [0]

        def mm(out_ps, lhsT, rhs, start, stop, reads, mark=None):
            pt_, pr_ = out_ps
            if mark is None:
                mark = stop
            S.op("pe", lambda e, o=pt_, l=lhsT, r=rhs, st=start, sp=stop: e.matmul(o, lhsT=l, rhs=r, start=st, stop=sp),
                 reads=reads, writes=[pr_], mark=mark)

        def act_op(out, in_, func, reads, writes, bias=None, scale=None):
            kw = {}
            if bias is not None:
                kw["bias"] = bias
            if scale is not None:
                kw["scale"] = scale
            S.op("act", lambda e: e.activation(out=out, in_=in_, func=func, **kw), reads=reads, writes=writes)

        def tt(eng, out, a, b, op, reads, writes):
            S.op(eng, lambda e: e.tensor_tensor(out=out, in0=a, in1=b, op=op), reads=reads, writes=writes)

        def stt(out, a, sc, b, op0, op1, reads, writes):
            S.op("dve", lambda e: e.scalar_tensor_tensor(out=out, in0=a, scalar=sc, in1=b, op0=op0, op1=op1),
                 reads=reads, writes=writes)

        def ts(eng, out, a, s1, s2, op0, op1, reads, writes):
            if op1 is None:
                S.op(eng, lambda e: e.tensor_scalar(out=out, in0=a, scalar1=s1, scalar2=None, op0=op0), reads=reads, writes=writes)
            else:
                S.op(eng, lambda e: e.tensor_scalar(out=out, in0=a, scalar1=s1, scalar2=s2, op0=op0, op1=op1), reads=reads, writes=writes)

        def recip(out, in_, reads, writes):
            S.op("dve", lambda e: e.reciprocal(out=out, in_=in_), reads=reads, writes=writes)

        def store_bf(dst_ap, n, producer):
            t, r = stg[stg_i[0] % 3]
            stg_i[0] += 1
            producer(t[:, 0:n], r)
            S.dma("sp", dst_ap, t[:, 0:n], r, reads=[r], store=True)

        for i_ in range(2):
            precast(0, i_)
        def load_const(t, r, src):
            S.dma("sp", t, src, r, writes=[r])
        load_const(small[:, :], small_r, IN["small"])
        load_const(masks[:, :], masks_r, IN["masks"])
        load_const(emask[:, :], emask_r, IN["emask"])
        load_const(eflag[:, :], eflag_r, IN["eflag"])
        nz_tok = S.dma("sp", nzt[:, :], IN["nonce"], nzt_r, writes=[nzt_r])
        load_const(csc[:, :], csc_r, IN["csc"])
        load_const(normw[:, :], normw_r, IN["normw"].rearrange("p l s c -> p (l s c)"))
        load_const(bada[:, :], bada_r, IN["bada"].rearrange("p l c -> p (l c)"))
        load_const(cin[:, :], cin_r, IN["cin"].rearrange("p k m -> p (k m)"))
        load_const(qkg[:, :], qkg_r, IN["qkg"].rearrange("p l t -> p (l t)"))
        load_const(sinkt[:, :], sink_r, IN["sink"].rearrange("p l h -> p (l h)"))
        load_const(pscale[:, :], pscale_r, IN["pscale"].rearrange("p l c -> p (l c)"))
        act_op(cond[:, :], cin[:, :], AF.Silu, [cin_r], [cond_r])
        act_op(sexp[:, :], sinkt[:, :], AF.Exp, [sink_r], [sexp_r])

        WA_CH = 2
        xgf = xg[:, :, :].rearrange("p k t -> p (k t)")
        half = [(xgf[:, 0:4096], Res("xgA", parent=xg_r)), (xgf[:, 4096:8192], Res("xgB", parent=xg_r))]
        it_ = 0
        for l in range(L):
            pst = PS[l % 2]
            for c0 in range(0, 72, WA_CH):
                wv, wv_r = half[it_ % 2]
                it_ += 1
                S.dma("sp", wv.rearrange("p (a f) -> p a f", a=WA_CH),
                      IN["wada"][l, c0:c0 + WA_CH].rearrange("a p k c -> p a (k c)"), wv_r, writes=[wv_r])
                for a in range(WA_CH):
                    cc_ = c0 + a
                    for kc in range(KC):
                        o = pst[0][:, cc_ * 2:cc_ * 2 + 2]
                        lh = wv[:, a * 2048 + kc * 128: a * 2048 + (kc + 1) * 128]
                        rh = cond[:, kc * 2:(kc + 1) * 2]
                        mm((o, pst[1]), lh, rh, kc == 0, kc == KC - 1, [wv_r, cond_r], mark=(kc == KC - 1))
            for m_ in range(2):
                src = pst[0][:, 0:144].rearrange("p (c m) -> p c m", m=2)[:, :, m_]
                dst = modh[:, l * 144:(l + 1) * 144].rearrange("p (c m) -> p c m", m=2)[:, :, m_]
                tt("dve", dst, src, bada[:, l * 72:(l + 1) * 72], ALU.add, [pst[1], bada_r], [modh_r])
        S.dma("sp", lambda pid: MODS[bass.ds(pid, 1), :, :].rearrange("o p x -> p (o x)"), modh[:, :], modh_r, reads=[modh_r], store=True)
        fsem = es.enter_context(nc.semaphore("fsem"))
        xsem = es.enter_context(nc.semaphore("xsem"))
        pubsem = es.enter_context(nc.semaphore("pubsem"))
        fetsem = es.enter_context(nc.semaphore("fetsem"))
        xb_cnt = [0]
        npub = [0]
        nfet = [0]
        def xbarrier(par=None):
            k = xb_cnt[0]
            xb_cnt[0] += 1
            if par is not None:
                waits = S.pending_waits("act")
                S.pending_stores = {}
                def pub(e, par=par):
                    pid = S.pid[id(e)]
                    e.dma_start(out=SHK[par][bass.ds(pid, 1)].rearrange("o g p t -> (o g) p t"), in_=KD[par][:, :, 0:HALF]).then_inc(pubsem, 16)
                    e.dma_start(out=SHV[par][bass.ds(pid, 1)].rearrange("o t c -> (o t) c"), in_=VD[par][0:HALF, :]).then_inc(pubsem, 16)
                    e.dma_start(out=SHU[par][bass.ds(pid, 1)].rearrange("o t c -> (o t) c"), in_=UD[par][0:HALF, :]).then_inc(pubsem, 16)
                    e.dma_start(out=SHP[par][bass.ds(pid, 1)].rearrange("o g p t -> (o g) p t"), in_=PD[par][:, :, 0:HALF]).then_inc(pubsem, 16)
                    return None
                S.custom("act", pub, waits)
                npub[0] += 4
                waits = [(pubsem, 16 * npub[0])]
            else:
                waits = S.pending_waits("sp")
                S.pending_stores = {}
            if k == 0:
                waits.append((nz_tok[2], nz_tok[3]))
            def fn(e, k=k, par=par):
                pid = S.pid[id(e)]
                e.dma_start(out=FLAGS[bass.ds(pid, 1), k:k + 1], in_=nzt[0:1, 0:1]).then_inc(fsem, 16)
                e.wait_ge(fsem, 16 * (k + 1))
                with e.register("nz%d" % k) as nz, e.register("fa%d" % k) as fa, e.register("fb%d" % k) as fb, \
                        e.register("df%d" % k) as df:
                    e.reg_load(nz, IN["nonce"][0:1, 0:1])
                    e.reg_mov(df, 1)
                    with e.While(df):
                        e.reg_load(fa, FLAGS[0:1, k:k + 1])
                        e.reg_load(fb, FLAGS[1:2, k:k + 1])
                        e.reg_sub(fa, fa, nz)
                        e.reg_sub(fb, fb, nz)
                        e.reg_alu(df, fa, fb, ALU.bitwise_or)
                e.dma_start(out=DUMMY[0:1, 0:1], in_=nzt[0:1, 0:1]).then_inc(xsem, 16)
                if par is not None:
                    oth = (pid + 1) % NCORES
                    e.dma_start(out=KD[par][:, :, HALF:SEQ], in_=SHK[par][bass.ds(oth, 1)].rearrange("o g p t -> (o g) p t")).then_inc(fetsem, 16)
                    e.dma_start(out=VD[par][HALF:SEQ, :], in_=SHV[par][bass.ds(oth, 1)].rearrange("o t c -> (o t) c")).then_inc(fetsem, 16)
                return None
            S.custom("sp", fn, waits)
            def fnp(e, par=par):
                if par is not None:
                    pid = S.pid[id(e)]
                    oth = (pid + 1) % NCORES
                    e.dma_start(out=UD[par][HALF:SEQ, :], in_=SHU[par][bass.ds(oth, 1)].rearrange("o t c -> (o t) c")).then_inc(fetsem, 16)
                    e.dma_start(out=PD[par][:, :, HALF:SEQ], in_=SHP[par][bass.ds(oth, 1)].rearrange("o g p t -> (o g) p t")).then_inc(fetsem, 16)
                return None
            S.custom("pool", fnp, [(xsem, 16 * (k + 1))])
            if par is not None:
                nfet[0] += 4
                S.custom("sp", None, [(fetsem, 16 * nfet[0])])
                S.custom("pool", None, [(fetsem, 16 * nfet[0])])

        xbarrier()
        for c_ in range(NCORES):
            S.dma("sp", modt[:, :].rearrange("p (l h x) -> p l h x", l=L, h=NCORES)[:, :, c_, :],
                  MODS[c_].rearrange("p (l x) -> p l x", l=L), mod_r, writes=[mod_r])
        def MOD(l, i, c, m_):
            o = ((l * 9 + i) * KC + c) * 2 + m_
            return modt[:, o:o + 1]
        for l in range(L):
            for sub in range(3):
                for m_ in range(2):
                    src = modt[:, (l * 9 + 3 * sub + 1) * 32:(l * 9 + 3 * sub + 2) * 32].rearrange("p (c m) -> p c m", m=2)[:, :, m_]
                    dst = amul[:, (l * 3 + sub) * 32:(l * 3 + sub + 1) * 32].rearrange("p (c m) -> p c m", m=2)[:, :, m_]
                    nw = normw[:, (l * 3 + sub) * KC:(l * 3 + sub + 1) * KC]
                    stt(dst, src, 1.0, nw, ALU.add, ALU.mult, [mod_r, normw_r], [amul_r])
            for s_ in range(2):
                src = modt[:, (l * 9 + 6 * s_ + 2) * 32:(l * 9 + 6 * s_ + 3) * 32]
                dst = hgt[:, (l * 2 + s_) * 32:(l * 2 + s_ + 1) * 32]
                ts("dve", dst, src, 0.5, None, ALU.mult, None, [mod_r], [hg_r])
        def AMUL(l, sub, c, m_):
            o = ((l * 3 + sub) * KC + c) * 2 + m_
            return amul[:, o:o + 1]
        def HG(l, s_, c, m_):
            o = ((l * 2 + s_) * KC + c) * 2 + m_
            return hgt[:, o:o + 1]

        def rms_rstd(n, nchunks_src, src_chunk, src_res, dim, sq_view, sq_res):
            pst = PS[6]
            for c in range(nchunks_src):
                if nchunks_src > 1 and c % 3 == 2:
                    tt("pool", sq_view(c), src_chunk(c), src_chunk(c), ALU.mult, src_res, [sq_res])
                else:
                    act_op(sq_view(c), src_chunk(c), AF.Square, src_res, [sq_res])
            for c in range(nchunks_src):
                mm((pst[0][:, 0:n], pst[1]), ones_ap, sq_view(c), c == 0, c == nchunks_src - 1,
                   [sq_res, small_r], mark=(c == nchunks_src - 1))
            act_op(tmpa[:, 0:n], pst[0][:, 0:n], AF.Sqrt, [pst[1]], [tmpa_r], bias=eps_ap, scale=1.0 / dim)
            recip(rstd[:, 0:n], tmpa[:, 0:n], [tmpa_r], [rstd_r])

        epst, eps_r = sb("epst", [128, 1], F32)
        S.op("dve", lambda e: e.memset(epst[:, :], EPS), writes=[eps_r])
        eps_ap = epst[:, 0:1]

        def modulate(l, sub, n, m_):
            sqv = lambda c: sqb[:, c * NT:c * NT + n]
            rms_rstd(n, KC, lambda c: xg[:, c, 0:n], [xg_r], float(D), sqv, sq_r)
            dtmp = [(tmpb, tmpb_r), (tmpc, tmpc_r), (silu_t, silu_r)]
            di = 0
            for c in range(KC):
                if c % 3 == 2:
                    eng, (tm, tm_r) = "pool", (stf, stf_r)
                else:
                    eng, (tm, tm_r) = "dve", dtmp[di % 3]
                    di += 1
                tt(eng, tm[:, 0:n], xg[:, c, 0:n], rstd[:, 0:n], ALU.mult, [xg_r, rstd_r], [tm_r])
                act_op(hb[:, c, 0:n], tm[:, 0:n], AF.Identity, [tm_r, amul_r, mod_r], [hb_r],
                       bias=MOD(l, 3 * sub, c, m_), scale=AMUL(l, sub, c, m_))

        def ffn(l, s_, n, m_):
            sub = 0 if s_ == 0 else 2
            modulate(l, sub, n, m_)
            for j in range(NJ):
                wt, wres = load_w(l, "gu%d" % s_, j)
                pg = PS[(j % 2) * 2]
                pu = PS[(j % 2) * 2 + 1]
                for a, pp in ((0, pg), (1, pu)):
                    for kc in range(KC):
                        mm((pp[0][:, 0:n], pp[1]), wt[:, (a * KC + kc) * 128:(a * KC + kc + 1) * 128], hb[:, kc, 0:n],
                           kc == 0, kc == KC - 1, [wres, hb_r], mark=(kc == KC - 1))
                act_op(silu_t[:, 0:n], pg[0][:, 0:n], AF.Silu, [pg[1]], [silu_r])
                tt("dve", actb[:, j * NT:j * NT + n], silu_t[:, 0:n], pu[0][:, 0:n], ALU.mult, [silu_r, pu[1]], [act_r])
            for c in range(KC):
                wt, wres = load_w(l, "dn%d" % s_, c)
                po = PS[4 + (c % 2)]
                for j in range(NJ):
                    mm((po[0][:, 0:n], po[1]), wt[:, j * 128:(j + 1) * 128], actb[:, j * NT:j * NT + n],
                       j == 0, j == NJ - 1, [wres, act_r], mark=(j == NJ - 1))
                stt(xg[:, c, 0:n], po[0][:, 0:n], HG(l, s_, c, m_), xg[:, c, 0:n], ALU.mult, ALU.add,
                    [po[1], hg_r, xg_r], [xg_r])

        def qk_head(l, hi, n, t0, is_ctx, praw, dst_ap):
            gidx = 0 if hi < 8 else 1
            act_op(stg_sq[0][:, 0:n], praw[0][:, 0:n], AF.Square, [praw[1]], [stg_sq[1]])
            pss = PS[6]
            mm((pss[0][:, 0:n], pss[1]), ones_ap, stg_sq[0][:, 0:n], True, True, [stg_sq[1], small_r])
            act_op(tmpa[:, 0:n], pss[0][:, 0:n], AF.Sqrt, [pss[1]], [tmpa_r], bias=eps_ap, scale=1.0 / 128.0)
            recip(rstd[:, 0:n], tmpa[:, 0:n], [tmpa_r], [rstd_r])
            g_ap = qkg[:, l * 2 + gidx:l * 2 + gidx + 1]
            if is_ctx:
                def prod(ap, r):
                    stt(ap, praw[0][:, 0:n], g_ap, rstd[:, 0:n], ALU.mult, ALU.mult, [praw[1], qkg_r, rstd_r], [r])
                store_bf(dst_ap, n, prod)
            else:
                stt(qn_t[0][:, 0:n], praw[0][:, 0:n], g_ap, rstd[:, 0:n], ALU.mult, ALU.mult,
                    [praw[1], qkg_r, rstd_r], [qn_t[1]])
                prt = PS[7]
                mm((prt[0][:, 0:n], prt[1]), perm_ap, qn_t[0][:, 0:n], True, True, [qn_t[1], small_r])
                tt("dve", tmpb[:, 0:n], qn_t[0][:, 0:n], rope[:, 0:n], ALU.mult, [qn_t[1], rope_r], [tmpb_r])
                tt("dve", tmpc[:, 0:n], prt[0][:, 0:n], rope[:, NT:NT + n], ALU.mult, [prt[1], rope_r], [tmpc_r])
                def prod(ap, r):
                    tt("dve", ap, tmpb[:, 0:n], tmpc[:, 0:n], ALU.add, [tmpb_r, tmpc_r], [r])
                store_bf(dst_ap, n, prod)

        stg_sq = sb("stg_sq", [128, NT], BF16)
        qn_t = sb("qn_t", [128, NT], BF16)
        uft, uft_r = sb("uft", [128, 4 * NT], BF16)

        def stage_A(l, t0, n, is_ctx):
            m_ = 1 if is_ctx else 0
            par = l % 2
            ffn(l, 0, n, m_)
            modulate(l, 1, n, m_)
            if not is_ctx:
                S.dma("sp", rope[:, :].rearrange("p (a t) -> p a t", a=2)[:, :, 0:n], IN["rope"][:, :, t0:t0 + n], rope_r, writes=[rope_r])
            kv_only = is_ctx and l == L - 1
            for hi in range(8 if kv_only else 0, 10):
                wt, wres = load_w(l, "wqk", hi)
                pp = PS[hi % 4]
                for kc in range(KC):
                    mm((pp[0][:, 0:n], pp[1]), wt[:, kc * 128:(kc + 1) * 128], hb[:, kc, 0:n], kc == 0, kc == KC - 1,
                       [wres, hb_r], mark=(kc == KC - 1))
                if hi < 8:
                    dst = QD[par][hi, :, t0:t0 + n]
                else:
                    dst = KD[par][hi - 8, :, lsl(t0, n, is_ctx)]
                qk_head(l, hi, n, t0, is_ctx, pp, dst)
            wt, wres = load_w(l, "wv", 0)
            for tb in range(n // 128):
                pp = PS[4 + (tb % 2)]
                for kc in range(KC):
                    mm((pp[0][:, 0:256], pp[1]), hb[:, kc, tb * 128:(tb + 1) * 128], wt[:, kc * 256:(kc + 1) * 256],
                       kc == 0, kc == KC - 1, [wres, hb_r], mark=(kc == KC - 1))
                def prod(ap, r, pp=pp):
                    act_op(ap, pp[0][:, 0:256], AF.Copy, [pp[1]], [r])
                store_bf(VD[par][lsl(t0 + tb * 128, 128, is_ctx), :], 256, prod)
            if kv_only:
                return
            for ci in range(8):
                wt, wres = load_w(l, "wfp", ci)
                pp = PS[ci % 4]
                for kc in range(KC):
                    mm((pp[0][:, 0:n], pp[1]), wt[:, kc * 128:(kc + 1) * 128], hb[:, kc, 0:n], kc == 0, kc == KC - 1,
                       [wres, hb_r], mark=(kc == KC - 1))
                if ci < 4:
                    act_op(uft[:, ci * NT:ci * NT + n], pp[0][:, 0:n], AF.Copy, [pp[1]], [uft_r])
                else:
                    S.op("dve", lambda e, pp=pp: e.tensor_copy(out=stf[:, 0:n], in_=pp[0][:, 0:n]), reads=[pp[1]], writes=[stf_r])
                    S.dma("sp", PD[par][ci - 4, :, lsl(t0, n, is_ctx)], stf[:, 0:n], stf_r,
                          reads=[stf_r], store=True)
            for tb in range(n // 128):
                for gi in range(4):
                    pp = PS[4 + ((tb * 4 + gi) % 2)]
                    mm((pp[0][:, 0:256], pp[1]), uft[:, gi * NT + tb * 128:gi * NT + (tb + 1) * 128], csc[:, :], True, True,
                       [uft_r, csc_r])
                    def prod(ap, r, pp=pp):
                        act_op(ap, pp[0][:, 0:256], AF.Copy, [pp[1]], [r])
                    store_bf(UD[par][lsl(t0 + tb * 128, 128, is_ctx), gi * 256:(gi + 1) * 256],
                             256, prod)
            for ci in range(48):
                wt, wres = load_w(l, "wg", ci)
                pp = PS[ci % 4]
                for kc in range(KC):
                    mm((pp[0][:, 0:n], pp[1]), wt[:, kc * 128:(kc + 1) * 128], hb[:, kc, 0:n], kc == 0, kc == KC - 1,
                       [wres, hb_r], mark=(kc == KC - 1))
                def prod(ap, r, pp=pp):
                    act_op(ap, pp[0][:, 0:n], AF.Sigmoid, [pp[1]], [r])
                store_bf(GD[par][ci, :, t0:t0 + n], n, prod)

        def attention(l, t0, n, is_ctx):
            par = l % 2
            S.dma("sp", qgt.rearrange("p (h t) -> p h t", h=8)[:, :, 0:n], QD[par][:, :, t0:t0 + n].rearrange("h p t -> p h t"),
                  qgt_r, writes=[qgt_r])
            ktv = kt.rearrange("p (g t) -> p g t", g=2)
            vtv = vt.rearrange("p (b c) -> p b c", b=8)
            S.dma("sp", ktv[:, :, 768:1024], KD[par][:, :, SEQ:SEQ + CTX].rearrange("g p t -> p g t"), kt_r, writes=[kt_r])
            S.dma("sp", vtv[:, 6:8, :], VD[par][SEQ:SEQ + CTX, :].rearrange("(b k) c -> k b c", b=2), vt_r, writes=[vt_r])
            if not is_ctx:
                if t0 == 0:
                    S.dma("sp", ktv[:, :, 0:128], KD[par][:, :, SEQ - 128:SEQ].rearrange("g p t -> p g t"), kt_r, writes=[kt_r])
                    S.dma("sp", vtv[:, 0:1, :], VD[par][SEQ - 128:SEQ, :].rearrange("(b k) c -> k b c", k=128), vt_r, writes=[vt_r])
                    S.dma("sp", ktv[:, :, 128:768], KD[par][:, :, 0:640].rearrange("g p t -> p g t"), kt_r, writes=[kt_r])
                    S.dma("sp", vtv[:, 1:6, :], VD[par][0:640, :].rearrange("(b k) c -> k b c", k=128), vt_r, writes=[vt_r])
                else:
                    S.dma("sp", ktv[:, :, 0:768], KD[par][:, :, t0 - 128:t0 + 640].rearrange("g p t -> p g t"), kt_r, writes=[kt_r])
                    S.dma("sp", vtv[:, 0:6, :], VD[par][t0 - 128:t0 + 640, :].rearrange("(b k) c -> k b c", k=128), vt_r, writes=[vt_r])
            qv = qgt.rearrange("p (h t) -> p h t", h=8)
            atv = at.rearrange("p (h t) -> p h t", h=8)
            it = 0
            for qb in range(n // 128):
                for g2 in range(2):
                    blocks = []
                    if not is_ctx:
                        first_q = (t0 == 0 and qb == 0)
                        last_q = (t0 + n == HALF and qb == n // 128 - 1)
                        mL = (emask, emask_r, 0) if first_q else (masks, masks_r, 0)
                        mR = (emask, emask_r, 1) if last_q else (masks, masks_r, 1)
                        blocks = [(qb, mL), (qb + 1, None), (qb + 2, mR)]
                        if False:
                            blocks = blocks[1:]
                    blocks += [(6, None), (7, None)]
                    po = PS[2 + 2 * (it % 2)]
                    pz = PS[3 + 2 * (it % 2)]
                    rhs_q = qv[:, 4 * g2:4 * g2 + 4, qb * 128:(qb + 1) * 128]
                    for bi, (slot, mi) in enumerate(blocks):
                        pst = PS[bi % 2]
                        mm((pst[0][:, :].rearrange("p (h t) -> p h t", h=4), pst[1]), ktv[:, g2, slot * 128:(slot + 1) * 128], rhs_q,
                           True, mi is None, [kt_r, qgt_r])
                        if mi is not None:
                            mm((pst[0][:, :], pst[1]), ident_ap, mi[0][:, mi[2] * 512:(mi[2] + 1) * 512], False, True, [small_r, mi[1]])
                        eb, eb_r = ebuf[bi % 2]
                        act_op(eb, pst[0][:, :], AF.Exp, [pst[1]], [eb_r], scale=att_scale)
                        first = bi == 0
                        last = bi == len(blocks) - 1
                        mm((po[0][:, :], po[1]), vtv[:, slot, g2 * 128:(g2 + 1) * 128], eb, first, last, [vt_r, eb_r], mark=True)
                        mm((pz[0][:, :], pz[1]), ones_ap, eb, first, last, [small_r, eb_r], mark=True)
                    for h4 in range(4):
                        hh = 4 * g2 + h4
                        ts("dve", rden[:, h4 * 128:(h4 + 1) * 128], pz[0][:, h4 * 128:(h4 + 1) * 128],
                           sexp[:, l * 8 + hh:l * 8 + hh + 1], None, ALU.add, None, [pz[1], sexp_r], [rden_r])
                    recip(rden[:, :], rden[:, :], [rden_r], [rden_r])
                    tt("dve", atv[:, 4 * g2:4 * g2 + 4, qb * 128:(qb + 1) * 128], po[0][:, :].rearrange("p (h t) -> p h t", h=4),
                       rden[:, :].rearrange("p (h t) -> p h t", h=4), ALU.mult, [po[1], rden_r], [at_r])
                    it += 1

        def fourier(l, t0, n, is_ctx):
            par = l % 2
            if is_ctx:
                nsc, sbase, rstride, cbase = CTX // 128, SEQ, SEQ // CTX, 0
            else:
                nsc, sbase, rstride, cbase = SEQ // 128, 0, 1, None
            k = 0
            for sc in range(nsc):
                ut, ur = ucs[k % 3]
                tb_, tr = tabt[k % 3]
                k += 1
                S.dma("sp", ut, UD[par][sbase + sc * 128:sbase + (sc + 1) * 128, :], ur, writes=[ur])
                tv = tb_.rearrange("p (a t) -> p a t", a=2)
                if rstride == 1:
                    src = IN["tab"][:, sc * 128:(sc + 1) * 128, t0:t0 + n].rearrange("a s t -> s a t")
                else:
                    src = IN["ctab"][:, sc * 128:(sc + 1) * 128, 0:n].rearrange("a s t -> s a t")
                S.dma("sp", tv[:, :, 0:n], src, tr, writes=[tr])
                for c4 in range(4):
                    for cs in range(2):
                        first = sc == 0 and cs == 0
                        last = sc == nsc - 1 and cs == 1
                        mm((PS[c4][0][:, 0:n], PS[c4][1]), ut[:, (c4 * 2 + cs) * 128:(c4 * 2 + cs + 1) * 128], tv[:, cs, 0:n],
                           first, last, [ur, tr], mark=(last or (c4 == 3 and cs == 1)))
            sc_f = float(np.sqrt(SEQ / CTX)) if is_ctx else 1.0
            for c4 in range(4):
                act_op(yt[:, c4 * NT:c4 * NT + n], PS[c4][0][:, 0:n], AF.Copy, [PS[c4][1]], [yt_r], scale=sc_f)

        def pool_mix(l, t0, n, is_ctx):
            par = l % 2
            W = n + 16
            uv = upt.rearrange("p (g t) -> p g t", g=4)
            if is_ctx:
                S.op("dve", lambda e: e.memset(upt[:, :], 0.0), writes=[upt_r])
                S.dma("sp", uv[:, :, 8:8 + n], PD[par][:, :, SEQ:SEQ + CTX].rearrange("g p t -> p g t"), upt_r, writes=[upt_r])
            elif t0 == 0:
                S.dma("sp", uv[:, :, 0:8], PD[par][:, :, SEQ - 8:SEQ].rearrange("g p t -> p g t"), upt_r, writes=[upt_r])
                S.dma("sp", uv[:, :, 8:W], PD[par][:, :, 0:n + 8].rearrange("g p t -> p g t"), upt_r, writes=[upt_r])
            else:
                S.dma("sp", uv[:, :, 0:W], PD[par][:, :, t0 - 8:t0 + n + 8].rearrange("g p t -> p g t"), upt_r, writes=[upt_r])
            if not is_ctx and t0 == 0:
                for gq in range(4):
                    ts("dve", uv[:, gq, 0:8], uv[:, gq, 0:8], eflag[:, 0:1], None, ALU.mult, None, [upt_r, eflag_r], [upt_r])
            if not is_ctx and t0 + n == HALF:
                for gq in range(4):
                    ts("dve", uv[:, gq, 8 + n:W], uv[:, gq, 8 + n:W], eflag[:, 1:2], None, ALU.mult, None, [upt_r, eflag_r], [upt_r])
            rsl = lsl(t0, n, is_ctx)
            S.dma("sp", rct.rearrange("p (g t) -> p g t", g=4)[:, :, 0:n],
                  IN["rc"][:, :, (HALF if is_ctx else t0):(HALF if is_ctx else t0) + n], rct_r, writes=[rct_r])
            rv = rct.rearrange("p (g t) -> p g t", g=4)
            for gi, w in enumerate((2, 4, 8, 16)):
                u = uv[:, gi, :]
                cur, cur_r, ln_ = u, upt_r, W
                step = 1
                bufs = [(pa, pa_r), (pb, pb_r)]
                bi = 0
                while step < w:
                    o, o_r = bufs[bi % 2]
                    bi += 1
                    nl = ln_ - step
                    tt("dve", o[:, 0:nl], cur[:, 0:nl], cur[:, step:step + nl], ALU.add, [cur_r], [o_r])
                    cur, cur_r, ln_ = o, o_r, nl
                    step *= 2
                st_ = 8 - w // 2
                o, o_r = bufs[bi % 2]
                tt("dve", o[:, 0:n], cur[:, st_:st_ + n], rv[:, gi, 0:n], ALU.mult, [cur_r, rct_r], [o_r])
                tt("dve", pt[:, gi * NT:gi * NT + n], o[:, 0:n], u[:, 8:8 + n], ALU.subtract, [o_r, upt_r], [pt_r])

        def merge_out(l, t0, n, is_ctx):
            par = l % 2
            m_ = 1 if is_ctx else 0
            atv = at.rearrange("p (h t) -> p h t", h=8)
            for c in range(KC):
                wt, wres = load_w(l, "wmg", c)
                gt, gr = gat[c % 2]
                gv = gt.rearrange("p (a t) -> p a t", a=3)
                S.dma("sp", gv[:, :, 0:n], GD[par][:, :, t0:t0 + n].rearrange("(a c) p t -> c p a t", a=3)[c], gr, writes=[gr])
                o3 = 0 if c % 2 == 0 else 4
                pA, pF, pP = PS[o3], PS[o3 + 1], PS[o3 + 2]
                for kc in range(8):
                    mm((pA[0][:, 0:n], pA[1]), wt[:, kc * 128:(kc + 1) * 128], atv[:, kc, 0:n], kc == 0, kc == 7, [wres, at_r], mark=(kc == 7))
                for kc in range(4):
                    mm((pF[0][:, 0:n], pF[1]), wt[:, (8 + kc) * 128:(9 + kc) * 128], yt[:, kc * NT:kc * NT + n], kc == 0, kc == 3,
                       [wres, yt_r], mark=(kc == 3))
                gi = c // 4
                mm((pP[0][:, 0:n], pP[1]), wt[:, 12 * 128:13 * 128], pt[:, gi * NT:gi * NT + n], True, True, [wres, pt_r])
                tt("dve", tmpa[:, 0:n], pA[0][:, 0:n], gv[:, 0, 0:n], ALU.mult, [pA[1], gr], [tmpa_r])
                tt("dve", tmpb[:, 0:n], pF[0][:, 0:n], gv[:, 1, 0:n], ALU.mult, [pF[1], gr], [tmpb_r])
                stt(tmpc[:, 0:n], pP[0][:, 0:n], pscale[:, l * KC + c:l * KC + c + 1], gv[:, 2, 0:n], ALU.mult, ALU.mult,
                    [pP[1], pscale_r, gr], [tmpc_r])
                tt("dve", tmpa[:, 0:n], tmpa[:, 0:n], tmpb[:, 0:n], ALU.add, [tmpa_r, tmpb_r], [tmpa_r])
                tt("dve", hb[:, c, 0:n], tmpa[:, 0:n], tmpc[:, 0:n], ALU.add, [tmpa_r, tmpc_r], [hb_r])
            for c in range(KC):
                wt, wres = load_w(l, "wout", c)
                po = PS[3 if c % 2 == 0 else 7]
                for kc in range(KC):
                    mm((po[0][:, 0:n], po[1]), wt[:, kc * 128:(kc + 1) * 128], hb[:, kc, 0:n], kc == 0, kc == KC - 1,
                       [wres, hb_r], mark=(kc == KC - 1))
                stt(xg[:, c, 0:n], po[0][:, 0:n], MOD(l, 5, c, m_), xg[:, c, 0:n], ALU.mult, ALU.add, [po[1], mod_r, xg_r], [xg_r])

        out_toks = []
        for sp_ in range(L + 1):
            if 1 <= sp_ + 1 < L:
                for i_ in range(2):
                    precast(sp_ + 1, i_)
            for gi_, (t0, n, is_ctx) in enumerate(groups):
                do_bc = sp_ >= 1 and not (is_ctx and sp_ - 1 == L - 1)
                do_a = sp_ < L
                if not (do_bc or do_a):
                    continue
                src = IN["xin"] if sp_ == 0 else XD
                S.dma("sp", xg[:, :, 0:n], src[:, :, t0:t0 + n].rearrange("k p t -> p k t"), xg_r, writes=[xg_r])
                if do_bc:
                    lb = sp_ - 1
                    attention(lb, t0, n, is_ctx)
                    fourier(lb, t0, n, is_ctx)
                    pool_mix(lb, t0, n, is_ctx)
                    merge_out(lb, t0, n, is_ctx)
                    ffn(lb, 1, n, 1 if is_ctx else 0)
                if do_a:
                    stage_A(sp_, t0, n, is_ctx)
                if sp_ == L:
                    tok = S.dma("sp", yout[:, :, t0:t0 + n].rearrange("k p t -> p k t"), xg[:, :, 0:n], xg_r, reads=[xg_r], store=True)
                    out_toks.append(tok)
                else:
                    S.dma("sp", XD[:, :, t0:t0 + n].rearrange("k p t -> p k t"), xg[:, :, 0:n], xg_r, reads=[xg_r], store=True)
            while pc_queue:
                precast(*pc_queue.pop(0))
            if sp_ < L:
                xbarrier(par=sp_ % 2)
        S.final_wait("sp", out_toks[-1:])

        with nc.Block() as block:
            S.emit(block)
    return nc


_SKIP = ("rope", "small", "masks", "csc", "tab", "rc")


def run(inputs, cfg):
    maps = _prep_inputs(inputs, cfg)
    shapes = {k: (v.shape, v.dtype) for k, v in maps[0].items()}
    nc = build_program(cfg, shapes)
    res = run_bass_kernel_spmd(nc, maps, core_ids=list(range(NCORES)))
    yT = np.concatenate([res.results[c]["yout"] for c in range(NCORES)], 2)
    SEQ = cfg["SEQ"]
    return np.ascontiguousarray(yT.reshape(D, SEQ).T.reshape(1, SEQ, D)).astype(np.float32)


def kernel(**inputs):
    return run(inputs, CFG)
```

```python
import numpy as np
import ml_dtypes
import concourse.bass as bass
import concourse.mybir as mybir
from concourse.bass_utils import run_bass_kernel_spmd

F32 = mybir.dt.float32
BF16 = mybir.dt.bfloat16
AF = mybir.ActivationFunctionType
ALU = mybir.AluOpType

CFG = dict(DEPTH=4, SEQ=8192, D_FF=5632)
D = 2048
KC = 16
CTX = 256
NT = 512
EPS = 1e-6
GRID_W = 64


class Res:
    def __init__(self, name, ap=None, parent=None):
        self.name = name
        self.ap = ap
        self.last_w = None
        self.readers = []
        self.parent = parent
        self.children = []
        self.dma_sem = None
        self.dma_cnt = 0
        if parent is not None:
            parent.children.append(self)


class Sched:
    COMPUTE = ("pe", "act", "dve", "pool")
    ALL = ("pe", "act", "dve", "pool", "sp")

    def __init__(self, nc, es):
        self.nc = nc
        self.es = es
        self.streams = {e: [] for e in self.ALL}
        self.esem = {e: es.enter_context(nc.semaphore("S_" + e)) for e in self.COMPUTE}
        self.tick = {e: 0 for e in self.COMPUTE}
        self.seen = {e: {} for e in self.ALL}
        self.pending_stores = {}
        self.nsem = 0
        self.pid = {}

    def _conflicts(self, r, write):
        toks = []
        def add(res):
            if res.last_w is not None:
                toks.append(res.last_w)
            if write:
                toks.extend(res.readers)
        add(r)
        if r.parent is not None:
            add(r.parent)
        for c in r.children:
            add(c)
        return toks

    def _waits(self, eng, reads, writes):
        need = {}
        for r in reads:
            for t in self._conflicts(r, False):
                self._need(eng, need, t)
        for r in writes:
            for t in self._conflicts(r, True):
                self._need(eng, need, t)
        out = []
        for key, (sem, val) in need.items():
            if self.seen[eng].get(key, 0) < val:
                self.seen[eng][key] = val
                out.append((sem, val))
        return out

    def _need(self, eng, need, tok):
        kind, key, sem, val = tok
        if kind == "eng" and key == eng:
            return
        if kind == "eng":
            assert val <= self.tick[key], "dangling PE mark dependency (would deadlock)"
        k = (kind, key)
        if k not in need or need[k][1] < val:
            need[k] = (sem, val)

    def _record(self, tok, reads, writes):
        for r in reads:
            r.readers.append(tok)
        for r in writes:
            r.last_w = tok
            r.readers = []

    def op(self, eng, fn, reads=(), writes=(), mark=True):
        waits = self._waits(eng, reads, writes)
        if mark:
            self.tick[eng] += 1
            tok = ("eng", eng, self.esem[eng], self.tick[eng])
            inc = (self.esem[eng], 1)
        else:
            tok = ("eng", eng, self.esem[eng], self.tick[eng] + 1)
            inc = None
        self.streams[eng].append((waits, fn, inc))
        self._record(tok, reads, writes)

    def dma(self, q, out, in_, sb, reads=(), writes=(), store=False, **kw):
        if sb.dma_sem is None:
            sb.dma_sem = self.es.enter_context(self.nc.semaphore("D_%d" % self.nsem))
            self.nsem += 1
        waits = self._waits(q, reads, writes)
        sb.dma_cnt += 16
        tok = ("dma", sb.name, sb.dma_sem, sb.dma_cnt)
        def _f(e, out=out, in_=in_, kw=kw):
            pid = self.pid[id(e)]
            o = out(pid) if callable(out) else out
            i = in_(pid) if callable(in_) else in_
            try:
                return e.dma_start(out=o, in_=i, **kw)
            except Exception:
                print("DMA FAIL", o, i, kw)
                raise
        self.streams[q].append((waits, _f, (sb.dma_sem, 16)))
        self._record(tok, reads, writes)
        if store:
            self.pending_stores[sb.name] = tok
        return tok

    def barrier(self):
        for q in ("sp", "pool"):
            waits = []
            for tok in self.pending_stores.values():
                kind, key, sem, val = tok
                k = (kind, key)
                if self.seen[q].get(k, 0) < val:
                    self.seen[q][k] = val
                    waits.append((sem, val))
            if waits:
                self.streams[q].append((waits, None, None))
        self.pending_stores = {}

    def custom(self, q, fn, waits=()):
        self.streams[q].append((list(waits), fn, None))

    def pending_waits(self, q):
        waits = []
        for tok in self.pending_stores.values():
            kind, key, sem, val = tok
            k = (kind, key)
            if self.seen[q].get(k, 0) < val:
                self.seen[q][k] = val
                waits.append((sem, val))
        return waits

    def final_wait(self, q, toks):
        waits = [(t[2], t[3]) for t in toks]
        self.streams[q].append((waits, None, None))

    def emit(self, block):
        nc = self.nc
        def run(stream, need_pid=False):
            def f(e):
                self.pid[id(e)] = e.partition_id() if need_pid else None
                for waits, fn, inc in stream:
                    for sem, val in waits:
                        e.wait_ge(sem, val)
                    if fn is not None:
                        ins = fn(e)
                        if inc is not None:
                            ins.then_inc(*inc)
            return f
        block.tensor(run(self.streams["pe"]))
        block.scalar(run(self.streams["act"], True))
        block.vector(run(self.streams["dve"]))
        block.gpsimd(run(self.streams["pool"], True))
        block.sync(run(self.streams["sp"], True))


def _fm_tiles(w, nchunk):
    K = w.shape[0]
    return np.ascontiguousarray(w.reshape(K // 128, 128, nchunk, 128).transpose(2, 1, 0, 3))


NPIECE = 136
NCORES = 2
PADT = 128


def _wlayout(NJ):
    items = [("gu0", 4096, NJ), ("dn0", NJ * 128, 16), ("gu1", 4096, NJ), ("dn1", NJ * 128, 16),
             ("wqk", 2048, 10), ("wv", 4096, 1), ("wfp", 2048, 8), ("wg", 2048, 48),
             ("wmg", 13 * 128, 16), ("wout", 2048, 16)]
    lay = {}
    off = 0
    for name, f, nt in items:
        lay[name] = (off, f, nt)
        off += 128 * f * nt
    q = 2048 * NPIECE * NCORES * 16
    tot = (off + q - 1) // q * q
    return lay, tot


def _constants(SEQ):
    TA = SEQ + CTX
    rows_n = SEQ // GRID_W
    rows = np.repeat(np.arange(rows_n), GRID_W).astype(np.float32)
    cols = np.tile(np.arange(GRID_W), rows_n).astype(np.float32)
    n_freq = 32
    freqs = (10000.0 ** (-np.arange(n_freq, dtype=np.float32) / n_freq)).astype(np.float32)
    ar = rows[None, :] * freqs[:, None]
    ac = cols[None, :] * freqs[:, None]
    cosT = np.concatenate([np.cos(ar), np.cos(ar), np.cos(ac), np.cos(ac)], 0).astype(np.float32)
    sinT = np.concatenate([-np.sin(ar), np.sin(ar), -np.sin(ac), np.sin(ac)], 0).astype(np.float32)
    rope = np.ascontiguousarray(np.stack([cosT, sinT], 1))
    perm = np.zeros((128, 128), np.float32)
    for d in range(128):
        src = d + 32 if (d % 64) < 32 else d - 32
        perm[src, d] = 1.0
    ident = np.eye(128, dtype=np.float32)
    ones = np.ones((128, 128), np.float32)
    kk = np.arange(128)[:, None]
    qq = np.arange(128)[None, :]
    mleft = np.where(kk >= qq, 0.0, -30000.0).astype(np.float32)
    mright = np.where(kk <= qq, 0.0, -30000.0).astype(np.float32)
    mats = np.stack([perm, ident, ones, np.tile(mleft, (1, 4))[:, :128] * 0], 0)
    small = np.concatenate([perm, ident, ones], 1).astype(ml_dtypes.bfloat16)
    masks = np.concatenate([np.tile(mleft, (1, 4)), np.tile(mright, (1, 4))], 1).astype(ml_dtypes.bfloat16)
    cc = np.arange(128)
    angc = 2 * np.pi * np.outer(cc, cc) / 128.0
    csc = (np.concatenate([np.cos(angc), np.sin(angc)], 1) / np.sqrt(128.0)).astype(ml_dtypes.bfloat16)
    s = np.arange(SEQ, dtype=np.int64)
    idx = np.outer(s, s) % SEQ
    ang = (2 * np.pi / SEQ) * idx
    tab = np.empty((2, SEQ, SEQ), ml_dtypes.bfloat16)
    tab[0] = (np.cos(ang) / np.sqrt(SEQ)).astype(ml_dtypes.bfloat16)
    tab[1] = (-np.sin(ang) / np.sqrt(SEQ)).astype(ml_dtypes.bfloat16)
    del ang, idx
    rc = np.zeros((4, TA), np.float32)
    for gi, w in enumerate((2, 4, 8, 16)):
        for (base, n) in ((0, SEQ), (SEQ, CTX)):
            t = np.arange(n)
            lo = np.clip(t - w // 2, 0, n)
            hi = np.clip(t + (w - w // 2), 0, n)
            rc[gi, base:base + n] = 1.0 / (hi - lo)
    rcb = np.ascontiguousarray(np.broadcast_to(rc[None], (128, 4, TA)))
    return dict(rope=rope, small=small, masks=masks, csc=csc, tab=tab, rc=rcb)


def _prep_inputs(inp, cfg):
    L, SEQ, DFF = cfg["DEPTH"], cfg["SEQ"], cfg["D_FF"]
    NJ = DFF // 128
    f = lambda a: np.asarray(a, dtype=np.float32)
    x = f(inp["x"])[0]
    ctx = f(inp["ctx"])[0]
    xa = np.concatenate([x, ctx], 0)
    m = {}
    m["xin"] = np.ascontiguousarray(xa.T.reshape(KC, 128, SEQ + CTX))
    cc = np.stack([f(inp["c"])[0], f(inp["c_ctx"])], -1)
    m["cin"] = np.ascontiguousarray(cc.reshape(KC, 128, 2).transpose(1, 0, 2))
    w_ada = f(inp["w_ada"])
    m["wada"] = np.stack([_fm_tiles(w_ada[l], 144) for l in range(L)], 0)
    m["bada"] = np.ascontiguousarray(f(inp["b_ada"])[:L].reshape(L, 144, 128).transpose(2, 0, 1))
    m["normw"] = np.ascontiguousarray(f(inp["norm_w"])[:L].reshape(L, 3, KC, 128).transpose(3, 0, 1, 2))
    g = f(inp["ffn_w_gate"]); u = f(inp["ffn_w_up"]); dn = f(inp["ffn_w_down"])
    gu = np.empty((L, 2, NJ, 128, 2, KC, 128), np.float32)
    dd = np.empty((L, 2, KC, 128, NJ, 128), np.float32)
    for l in range(L):
        for s in range(2):
            gu[l, s, :, :, 0] = _fm_tiles(g[l, s], NJ)
            gu[l, s, :, :, 1] = _fm_tiles(u[l, s], NJ)
            dd[l, s] = dn[l, s].reshape(NJ, 128, KC, 128).transpose(2, 1, 0, 3)
    m["wgu"] = gu
    m["wdn"] = dd
    w_in = f(inp["w_in"])
    m["wqk"] = np.stack([_fm_tiles(w_in[l][:, 0:1280], 10) for l in range(L)], 0)
    m["wv"] = np.stack([np.ascontiguousarray(w_in[l][:, 1280:1536].reshape(KC, 128, 256).transpose(1, 0, 2)) for l in range(L)], 0)
    m["wfp"] = np.stack([_fm_tiles(w_in[l][:, 1536:2560], 8) for l in range(L)], 0)
    m["wg"] = np.stack([_fm_tiles(w_in[l][:, 2560:8704], 48) for l in range(L)], 0)
    wao = f(inp["w_attn_o"]); wfo = f(inp["w_fourier"]); wpl = f(inp["w_pool"])
    mm = np.empty((L, KC, 128, 13, 128), np.float32)
    for l in range(L):
        mm[l, :, :, 0:8] = _fm_tiles(wao[l], 16)
        mm[l, :, :, 8:12] = _fm_tiles(wfo[l], 16)
        mm[l, :, :, 12] = wpl[l].reshape(4, 128, 4, 128).transpose(0, 2, 1, 3).reshape(16, 128, 128)
    m["wmg"] = mm
    m["wout"] = np.stack([_fm_tiles(f(inp["w_out"])[l], 16) for l in range(L)], 0)
    qg = f(inp["q_gain"])[:L]; kg = f(inp["k_gain"])[:L]
    m["qkg"] = np.ascontiguousarray(np.stack([qg, kg], -1).transpose(1, 0, 2))
    m["sink"] = np.ascontiguousarray(np.broadcast_to(f(inp["sink"])[:L][None], (128, L, 8)))
    m["pscale"] = np.ascontiguousarray(f(inp["pool_scale"])[:L].reshape(L, KC, 128).transpose(2, 0, 1))
    lay, tot = _wlayout(NJ)
    blob = np.zeros((L, tot), np.float32)
    for l in range(L):
        parts = [gu[l, 0], dd[l, 0], gu[l, 1], dd[l, 1], m["wqk"][l], m["wv"][l], m["wfp"][l], m["wg"][l],
                 m["wmg"][l], m["wout"][l]]
        o = 0
        for p_ in parts:
            blob[l, o:o + p_.size] = p_.ravel()
            o += p_.size
    for k in ("wgu", "wdn", "wqk", "wv", "wfp", "wg", "wmg", "wout"):
        del m[k]
    m.update(_constants(SEQ))
    HALF = SEQ // NCORES
    HW = tot // NCORES
    kk = np.arange(128)[:, None]
    qq = np.arange(128)[None, :]
    mleft = np.tile(np.where(kk >= qq, 0.0, -30000.0), (1, 4)).astype(np.float32)
    mright = np.tile(np.where(kk <= qq, 0.0, -30000.0), (1, 4)).astype(np.float32)
    dead = np.full((128, 512), -30000.0, np.float32)
    nonce = np.array([[int(np.random.randint(1, 2 ** 30)), 0, 0, 0]], np.int32)
    xin = m.pop("xin"); wada = m.pop("wada"); bada = m.pop("bada")
    maps = []
    for c in range(NCORES):
        mc = dict(m)
        mc["xin"] = np.ascontiguousarray(np.concatenate([xin[:, :, c * HALF:(c + 1) * HALF], xin[:, :, SEQ:]], 2))
        mc["wada"] = np.ascontiguousarray(wada[:, c * 72:(c + 1) * 72])
        mc["bada"] = np.ascontiguousarray(bada[:, :, c * 72:(c + 1) * 72])
        mc["wblob"] = np.ascontiguousarray(blob[:, c * HW:(c + 1) * HW])
        eL = dead if c == 0 else mleft
        eR = dead if c == NCORES - 1 else mright
        mc["emask"] = np.concatenate([eL, eR], 1).astype(ml_dtypes.bfloat16)
        mc["eflag"] = np.ascontiguousarray(np.broadcast_to(
            np.array([[0.0 if c == 0 else 1.0, 0.0 if c == NCORES - 1 else 1.0]], np.float32), (128, 2)))
        mc["rope"] = np.ascontiguousarray(m["rope"][:, :, c * HALF:(c + 1) * HALF])
        mc["rc"] = np.ascontiguousarray(np.concatenate([m["rc"][:, :, c * HALF:(c + 1) * HALF], m["rc"][:, :, SEQ:]], 2))
        sg = (np.arange(SEQ) + c * HALF) % SEQ
        mc["tab"] = np.ascontiguousarray(m["tab"][:, sg, c * HALF:(c + 1) * HALF])
        mc["ctab"] = np.ascontiguousarray(m["tab"][:, ::SEQ // CTX, 0:CTX][:, 0:CTX])
        mc["nonce"] = nonce
        maps.append(mc)
    return maps


def build_program(cfg, shapes):
    from contextlib import ExitStack
    L, SEQ, DFF = cfg["DEPTH"], cfg["SEQ"], cfg["D_FF"]
    NJ = DFF // 128
    TA = SEQ + CTX
    NXG = SEQ // NT
    HALF = SEQ // NCORES
    NXGL = NXG // NCORES
    assert NXG % NCORES == 0
    TL = HALF + CTX
    TP = SEQ + CTX + 3 * PADT
    CTXP = SEQ + 2 * PADT
    groups = [(g * NT, NT, False) for g in range(NXGL)] + [(HALF, CTX, True)]
    NB = SEQ // 128

    def lsl(t0l, n, is_ctx):
        st = SEQ + (t0l - HALF) if is_ctx else t0l
        return slice(st, st + n)
    att_scale = 128.0 ** -0.5

    nc = bass.Bass("TRN2", target_bir_lowering=False)
    dt_of = {np.dtype(np.float32): F32, np.dtype(ml_dtypes.bfloat16): BF16, np.dtype(np.int32): mybir.dt.int32}
    IN = {k: nc.dram_tensor(k, list(shp), dt_of[np.dtype(dtp)], kind="ExternalInput").ap()
          for k, (shp, dtp) in shapes.items()}
    yout = nc.dram_tensor("yout", [KC, 128, HALF], F32, kind="ExternalOutput").ap()

    def dscr(name, shape, dt):
        return nc.dram_tensor(name, shape, dt, kind="Internal").ap()
    def dshr(name, shape, dt):
        return nc.dram_tensor(name, shape, dt, kind="Internal", addr_space="Shared").ap()
    XD = dscr("XD", [KC, 128, TL], F32)
    QD = [dscr("QD%d" % i, [8, 128, TL], BF16) for i in range(2)]
    GD = [dscr("GD%d" % i, [48, 128, TL], BF16) for i in range(2)]
    KD = [dscr("KD%d" % i, [2, 128, TA], BF16) for i in range(2)]
    VD = [dscr("VD%d" % i, [TA, 256], BF16) for i in range(2)]
    UD = [dscr("UD%d" % i, [TA, 1024], BF16) for i in range(2)]
    PD = [dscr("PD%d" % i, [4, 128, TA], F32) for i in range(2)]
    SHK = [dshr("SHK%d" % i, [NCORES, 2, 128, HALF], BF16) for i in range(2)]
    SHV = [dshr("SHV%d" % i, [NCORES, HALF, 256], BF16) for i in range(2)]
    SHU = [dshr("SHU%d" % i, [NCORES, HALF, 1024], BF16) for i in range(2)]
    SHP = [dshr("SHP%d" % i, [NCORES, 4, 128, HALF], F32) for i in range(2)]
    LAY, WTOT = _wlayout(NJ)
    HW = WTOT // NCORES
    WB2 = [dshr("WB%d" % i, [NCORES, HW // 2048, 2048], BF16) for i in range(L)]
    WB = [w.rearrange("n r c -> (n r c)") for w in WB2]
    MODS = dshr("MODS", [NCORES, 128, L * 144], F32)
    FLAGS = dshr("FLAGS", [NCORES, 16], mybir.dt.int32)
    DUMMY = dscr("DUMMY", [1, 4], mybir.dt.int32)

    es = ExitStack()
    with es:
        S = Sched(nc, es)
        def sb(name, shape, dt, parent=None):
            t = es.enter_context(nc.sbuf_tensor("s_" + name, shape, dt))
            return t, Res(name, parent=parent)
        def ps(name):
            t = es.enter_context(nc.psum_tensor(name, [128, 512], F32))
            return t, Res(name)

        PS = [ps("ps%d" % i) for i in range(8)]
        xg, xg_r = sb("xg", [128, KC, NT], F32)
        hb, hb_r = sb("hb", [128, KC, NT], BF16)
        NJA = max(NJ, 61)
        actb, act_r = sb("actb", [128, NJA * NT], BF16)
        sqb, sq_r = sb("sqb", [128, KC * NT], BF16)
        WSL = 3
        WBYTES = max(NJ * 128, 4096)
        wr = [sb("wr%d" % i, [128, WBYTES], BF16) for i in range(WSL)]
        wr_i = [0]
        modt, mod_r = sb("modt", [128, L * 9 * KC * 2], F32)
        amul, amul_r = sb("amul", [128, L * 3 * KC * 2], F32)
        hgt, hg_r = sb("hgt", [128, L * 2 * KC * 2], F32)
        normw, normw_r = sb("normw", [128, L * 3 * KC], F32)
        bada, bada_r = sb("bada", [128, L * 72], F32)
        modh, modh_r = sb("modh", [128, L * 144], F32)
        emask, emask_r = sb("emask", [128, 1024], BF16)
        eflag, eflag_r = sb("eflag", [128, 2], F32)
        nzt, nzt_r = sb("nzt", [1, 4], mybir.dt.int32)
        zt, zt_r = sb("zt", [128, 256], F32)
        cin, cin_r = sb("cin", [128, KC * 2], F32)
        cond, cond_r = sb("cond", [128, KC * 2], F32)
        qkg, qkg_r = sb("qkg", [128, L * 2], F32)
        sinkt, sink_r = sb("sinkt", [128, L * 8], F32)
        pscale, pscale_r = sb("pscale", [128, L * KC], F32)
        small, small_r = sb("small", [128, 384], BF16)
        masks, masks_r = sb("masks", [128, 1024], BF16)
        csc, csc_r = sb("csc", [128, 256], BF16)
        rope, rope_r = sb("rope", [128, 2 * NT], F32)
        rstd, rstd_r = sb("rstd", [128, NT], F32)
        tmpa, tmpa_r = sb("tmpa", [128, NT], F32)
        tmpb, tmpb_r = sb("tmpb", [128, NT], F32)
        tmpc, tmpc_r = sb("tmpc", [128, NT], F32)
        silu_t, silu_r = sb("silu_t", [128, NT], F32)
        stg = [sb("stg%d" % i, [128, NT], BF16) for i in range(3)]
        stg_i = [0]
        stf, stf_r = sb("stf", [128, NT], F32)

        perm_ap = small[:, 0:128]
        ident_ap = small[:, 128:256]
        ones_ap = small[:, 256:384]

        off = [0]
        def carve(name, n, dt=BF16):
            nb = n * (2 if dt == F32 else 1)
            assert off[0] + nb <= NJA * NT, "act scratch overflow"
            v = actb[:, off[0]:off[0] + nb]
            if dt == F32:
                v = v.bitcast(F32)
            off[0] += nb
            return v, Res(name, parent=act_r)
        qgt, qgt_r = carve("qgt", 8 * NT)
        kt, kt_r = carve("kt", 2 * 8 * 128)
        vt, vt_r = carve("vt", 8 * 256)
        rden, rden_r = carve("rden", NT, F32)
        at, at_r = carve("at", 8 * NT)
        yt, yt_r = carve("yt", 4 * NT)
        pt, pt_r = carve("pt", 4 * NT)
        upt, upt_r = carve("upt", 4 * (NT + 16), F32)
        pa, pa_r = carve("pa", NT + 16, F32)
        pb, pb_r = carve("pb", NT + 16, F32)
        rct, rct_r = carve("rct", 4 * NT, F32)
        gat = [carve("gat%d" % i, 3 * NT) for i in range(2)]
        soff = [0]
        def carve_s(name, n):
            assert soff[0] + n <= KC * NT
            v = sqb[:, soff[0]:soff[0] + n]
            soff[0] += n
            return v, Res(name, parent=sq_r)
        ucs = [carve_s("ucs%d" % i, 1024) for i in range(3)]
        tabt = [carve_s("tab%d" % i, 1024) for i in range(3)]
        ebuf = [carve_s("e%d" % i, NT) for i in range(2)]
        sexp, sexp_r = sb("sexp", [128, L * 8], F32)

        def next_w():
            i = wr_i[0] % WSL
            wr_i[0] += 1
            return wr[i]

        pc_res = [Res("pc%d" % l) for l in range(L)]
        PIECE = HW // 2

        def precast(l, i):
            a, b = i * PIECE, (i + 1) * PIECE
            S.dma("pool", lambda pid: WB2[l][bass.ds(pid, 1), a // 2048:a // 2048 + PIECE // 2048, :].rearrange("o r c -> (o r) c"),
                  IN["wblob"][l, a:b].rearrange("(r c) -> r c", c=2048),
                  pc_res[l], writes=[pc_res[l]], store=True, max_dma_last_dim=8192)

        def load_w(l, name, idx):
            base, f, nt = LAY[name]
            assert idx < nt
            off_ = base + idx * 128 * f
            t, r = next_w()
            S.dma("pool", t[:, 0:f], WB[l][off_:off_ + 128 * f].rearrange("(p f) -> p f", f=f), r,
                  reads=[pc_res[l]], writes=[r])
            lw_cnt[0] += 1
            if False:
                precast(*pc_queue.pop(0))
            return t, r

        pc_queue = []
        lw_cnt = [0]

        def mm(out_ps, lhsT, rhs, start, stop, reads, mark=None):
            pt_, pr_ = out_ps
            if mark is None:
                mark = stop
            S.op("pe", lambda e, o=pt_, l=lhsT, r=rhs, st=start, sp=stop: e.matmul(o, lhsT=l, rhs=r, start=st, stop=sp),
                 reads=reads, writes=[pr_], mark=mark)

        def act_op(out, in_, func, reads, writes, bias=None, scale=None):
            kw = {}
            if bias is not None:
                kw["bias"] = bias
            if scale is not None:
                kw["scale"] = scale
            S.op("act", lambda e: e.activation(out=out, in_=in_, func=func, **kw), reads=reads, writes=writes)

        def tt(eng, out, a, b, op, reads, writes):
            S.op(eng, lambda e: e.tensor_tensor(out=out, in0=a, in1=b, op=op), reads=reads, writes=writes)

        def stt(out, a, sc, b, op0, op1, reads, writes):
            S.op("dve", lambda e: e.scalar_tensor_tensor(out=out, in0=a, scalar=sc, in1=b, op0=op0, op1=op1),
                 reads=reads, writes=writes)

        def ts(eng, out, a, s1, s2, op0, op1, reads, writes):
            if op1 is None:
                S.op(eng, lambda e: e.tensor_scalar(out=out, in0=a, scalar1=s1, scalar2=None, op0=op0), reads=reads, writes=writes)
            else:
                S.op(eng, lambda e: e.tensor_scalar(out=out, in0=a, scalar1=s1, scalar2=s2, op0=op0, op1=op1), reads=reads, writes=writes)

        def recip(out, in_, reads, writes):
            S.op("dve", lambda e: e.reciprocal(out=out, in_=in_), reads=reads, writes=writes)

        def store_bf(dst_ap, n, producer):
            t, r = stg[stg_i[0] % 3]
            stg_i[0] += 1
            producer(t[:, 0:n], r)
            S.dma("sp", dst_ap, t[:, 0:n], r, reads=[r], store=True)

        for i_ in range(2):
            precast(0, i_)
        def load_const(t, r, src):
            S.dma("sp", t, src, r, writes=[r])
        load_const(small[:, :], small_r, IN["small"])
        load_const(masks[:, :], masks_r, IN["masks"])
        load_const(emask[:, :], emask_r, IN["emask"])
        load_const(eflag[:, :], eflag_r, IN["eflag"])
        nz_tok = S.dma("sp", nzt[:, :], IN["nonce"], nzt_r, writes=[nzt_r])
        load_const(csc[:, :], csc_r, IN["csc"])
        load_const(normw[:, :], normw_r, IN["normw"].rearrange("p l s c -> p (l s c)"))
        load_const(bada[:, :], bada_r, IN["bada"].rearrange("p l c -> p (l c)"))
        load_const(cin[:, :], cin_r, IN["cin"].rearrange("p k m -> p (k m)"))
        load_const(qkg[:, :], qkg_r, IN["qkg"].rearrange("p l t -> p (l t)"))
        load_const(sinkt[:, :], sink_r, IN["sink"].rearrange("p l h -> p (l h)"))
        load_const(pscale[:, :], pscale_r, IN["pscale"].rearrange("p l c -> p (l c)"))
        act_op(cond[:, :], cin[:, :], AF.Silu, [cin_r], [cond_r])
        act_op(sexp[:, :], sinkt[:, :], AF.Exp, [sink_r], [sexp_r])

        WA_CH = 2
        xgf = xg[:, :, :].rearrange("p k t -> p (k t)")
        half = [(xgf[:, 0:4096], Res("xgA", parent=xg_r)), (xgf[:, 4096:8192], Res("xgB", parent=xg_r))]
        it_ = 0
        for l in range(L):
            pst = PS[l % 2]
            for c0 in range(0, 72, WA_CH):
                wv, wv_r = half[it_ % 2]
                it_ += 1
                S.dma("sp", wv.rearrange("p (a f) -> p a f", a=WA_CH),
                      IN["wada"][l, c0:c0 + WA_CH].rearrange("a p k c -> p a (k c)"), wv_r, writes=[wv_r])
                for a in range(WA_CH):
                    cc_ = c0 + a
                    for kc in range(KC):
                        o = pst[0][:, cc_ * 2:cc_ * 2 + 2]
                        lh = wv[:, a * 2048 + kc * 128: a * 2048 + (kc + 1) * 128]
                        rh = cond[:, kc * 2:(kc + 1) * 2]
                        mm((o, pst[1]), lh, rh, kc == 0, kc == KC - 1, [wv_r, cond_r], mark=(kc == KC - 1))
            for m_ in range(2):
                src = pst[0][:, 0:144].rearrange("p (c m) -> p c m", m=2)[:, :, m_]
                dst = modh[:, l * 144:(l + 1) * 144].rearrange("p (c m) -> p c m", m=2)[:, :, m_]
                tt("dve", dst, src, bada[:, l * 72:(l + 1) * 72], ALU.add, [pst[1], bada_r], [modh_r])
        S.dma("sp", lambda pid: MODS[bass.ds(pid, 1), :, :].rearrange("o p x -> p (o x)"), modh[:, :], modh_r, reads=[modh_r], store=True)
        fsem = es.enter_context(nc.semaphore("fsem"))
        xsem = es.enter_context(nc.semaphore("xsem"))
        pubsem = es.enter_context(nc.semaphore("pubsem"))
        fetsem = es.enter_context(nc.semaphore("fetsem"))
        xb_cnt = [0]
        npub = [0]
        nfet = [0]
        def xbarrier(par=None):
            k = xb_cnt[0]
            xb_cnt[0] += 1
            if par is not None:
                waits = S.pending_waits("act")
                S.pending_stores = {}
                def pub(e, par=par):
                    pid = S.pid[id(e)]
                    e.dma_start(out=SHK[par][bass.ds(pid, 1)].rearrange("o g p t -> (o g) p t"), in_=KD[par][:, :, 0:HALF]).then_inc(pubsem, 16)
                    e.dma_start(out=SHV[par][bass.ds(pid, 1)].rearrange("o t c -> (o t) c"), in_=VD[par][0:HALF, :]).then_inc(pubsem, 16)
                    e.dma_start(out=SHU[par][bass.ds(pid, 1)].rearrange("o t c -> (o t) c"), in_=UD[par][0:HALF, :]).then_inc(pubsem, 16)
                    e.dma_start(out=SHP[par][bass.ds(pid, 1)].rearrange("o g p t -> (o g) p t"), in_=PD[par][:, :, 0:HALF]).then_inc(pubsem, 16)
                    return None
                S.custom("act", pub, waits)
                npub[0] += 4
                waits = [(pubsem, 16 * npub[0])]
            else:
                waits = S.pending_waits("sp")
                S.pending_stores = {}
            if k == 0:
                waits.append((nz_tok[2], nz_tok[3]))
            def fn(e, k=k, par=par):
                pid = S.pid[id(e)]
                e.dma_start(out=FLAGS[bass.ds(pid, 1), k:k + 1], in_=nzt[0:1, 0:1]).then_inc(fsem, 16)
                e.wait_ge(fsem, 16 * (k + 1))
                with e.register("nz%d" % k) as nz, e.register("fa%d" % k) as fa, e.register("fb%d" % k) as fb, \
                        e.register("df%d" % k) as df:
                    e.reg_load(nz, IN["nonce"][0:1, 0:1])
                    e.reg_mov(df, 1)
                    with e.While(df):
                        e.reg_load(fa, FLAGS[0:1, k:k + 1])
                        e.reg_load(fb, FLAGS[1:2, k:k + 1])
                        e.reg_sub(fa, fa, nz)
                        e.reg_sub(fb, fb, nz)
                        e.reg_alu(df, fa, fb, ALU.bitwise_or)
                e.dma_start(out=DUMMY[0:1, 0:1], in_=nzt[0:1, 0:1]).then_inc(xsem, 16)
                if par is not None:
                    oth = (pid + 1) % NCORES
                    e.dma_start(out=KD[par][:, :, HALF:SEQ], in_=SHK[par][bass.ds(oth, 1)].rearrange("o g p t -> (o g) p t")).then_inc(fetsem, 16)
                    e.dma_start(out=VD[par][HALF:SEQ, :], in_=SHV[par][bass.ds(oth, 1)].rearrange("o t c -> (o t) c")).then_inc(fetsem, 16)
                return None
            S.custom("sp", fn, waits)
            def fnp(e, par=par):
                if par is not None:
                    pid = S.pid[id(e)]
                    oth = (pid + 1) % NCORES
                    e.dma_start(out=UD[par][HALF:SEQ, :], in_=SHU[par][bass.ds(oth, 1)].rearrange("o t c -> (o t) c")).then_inc(fetsem, 16)
                    e.dma_start(out=PD[par][:, :, HALF:SEQ], in_=SHP[par][bass.ds(oth, 1)].rearrange("o g p t -> (o g) p t")).then_inc(fetsem, 16)
                return None
            S.custom("pool", fnp, [(xsem, 16 * (k + 1))])
            if par is not None:
                nfet[0] += 4
                S.custom("sp", None, [(fetsem, 16 * nfet[0])])
                S.custom("pool", None, [(fetsem, 16 * nfet[0])])

        xbarrier()
        for c_ in range(NCORES):
            S.dma("sp", modt[:, :].rearrange("p (l h x) -> p l h x", l=L, h=NCORES)[:, :, c_, :],
                  MODS[c_].rearrange("p (l x) -> p l x", l=L), mod_r, writes=[mod_r])
        def MOD(l, i, c, m_):
            o = ((l * 9 + i) * KC + c) * 2 + m_
            return modt[:, o:o + 1]
        for l in range(L):
            for sub in range(3):
                for m_ in range(2):
                    src = modt[:, (l * 9 + 3 * sub + 1) * 32:(l * 9 + 3 * sub + 2) * 32].rearrange("p (c m) -> p c m", m=2)[:, :, m_]
                    dst = amul[:, (l * 3 + sub) * 32:(l * 3 + sub + 1) * 32].rearrange("p (c m) -> p c m", m=2)[:, :, m_]
                    nw = normw[:, (l * 3 + sub) * KC:(l * 3 + sub + 1) * KC]
                    stt(dst, src, 1.0, nw, ALU.add, ALU.mult, [mod_r, normw_r], [amul_r])
            for s_ in range(2):
                src = modt[:, (l * 9 + 6 * s_ + 2) * 32:(l * 9 + 6 * s_ + 3) * 32]
                dst = hgt[:, (l * 2 + s_) * 32:(l * 2 + s_ + 1) * 32]
                ts("dve", dst, src, 0.5, None, ALU.mult, None, [mod_r], [hg_r])
        def AMUL(l, sub, c, m_):
            o = ((l * 3 + sub) * KC + c) * 2 + m_
            return amul[:, o:o + 1]
        def HG(l, s_, c, m_):
            o = ((l * 2 + s_) * KC + c) * 2 + m_
            return hgt[:, o:o + 1]

        def rms_rstd(n, nchunks_src, src_chunk, src_res, dim, sq_view, sq_res):
            pst = PS[6]
            for c in range(nchunks_src):
                if nchunks_src > 1 and c % 3 == 2:
                    tt("pool", sq_view(c), src_chunk(c), src_chunk(c), ALU.mult, src_res, [sq_res])
                else:
                    act_op(sq_view(c), src_chunk(c), AF.Square, src_res, [sq_res])
            for c in range(nchunks_src):
                mm((pst[0][:, 0:n], pst[1]), ones_ap, sq_view(c), c == 0, c == nchunks_src - 1,
                   [sq_res, small_r], mark=(c == nchunks_src - 1))
            act_op(tmpa[:, 0:n], pst[0][:, 0:n], AF.Sqrt, [pst[1]], [tmpa_r], bias=eps_ap, scale=1.0 / dim)
            recip(rstd[:, 0:n], tmpa[:, 0:n], [tmpa_r], [rstd_r])

        epst, eps_r = sb("epst", [128, 1], F32)
        S.op("dve", lambda e: e.memset(epst[:, :], EPS), writes=[eps_r])
        eps_ap = epst[:, 0:1]

        def modulate(l, sub, n, m_):
            sqv = lambda c: sqb[:, c * NT:c * NT + n]
            rms_rstd(n, KC, lambda c: xg[:, c, 0:n], [xg_r], float(D), sqv, sq_r)
            dtmp = [(tmpb, tmpb_r), (tmpc, tmpc_r), (silu_t, silu_r)]
            di = 0
            for c in range(KC):
                if c % 3 == 2:
                    eng, (tm, tm_r) = "pool", (stf, stf_r)
                else:
                    eng, (tm, tm_r) = "dve", dtmp[di % 3]
                    di += 1
                tt(eng, tm[:, 0:n], xg[:, c, 0:n], rstd[:, 0:n], ALU.mult, [xg_r, rstd_r], [tm_r])
                act_op(hb[:, c, 0:n], tm[:, 0:n], AF.Identity, [tm_r, amul_r, mod_r], [hb_r],
                       bias=MOD(l, 3 * sub, c, m_), scale=AMUL(l, sub, c, m_))

        def ffn(l, s_, n, m_):
            sub = 0 if s_ == 0 else 2
            modulate(l, sub, n, m_)
            for j in range(NJ):
                wt, wres = load_w(l, "gu%d" % s_, j)
                pg = PS[(j % 2) * 2]
                pu = PS[(j % 2) * 2 + 1]
                for a, pp in ((0, pg), (1, pu)):
                    for kc in range(KC):
                        mm((pp[0][:, 0:n], pp[1]), wt[:, (a * KC + kc) * 128:(a * KC + kc + 1) * 128], hb[:, kc, 0:n],
                           kc == 0, kc == KC - 1, [wres, hb_r], mark=(kc == KC - 1))
                act_op(silu_t[:, 0:n], pg[0][:, 0:n], AF.Silu, [pg[1]], [silu_r])
                tt("dve", actb[:, j * NT:j * NT + n], silu_t[:, 0:n], pu[0][:, 0:n], ALU.mult, [silu_r, pu[1]], [act_r])
            for c in range(KC):
                wt, wres = load_w(l, "dn%d" % s_, c)
                po = PS[4 + (c % 2)]
                for j in range(NJ):
                    mm((po[0][:, 0:n], po[1]), wt[:, j * 128:(j + 1) * 128], actb[:, j * NT:j * NT + n],
                       j == 0, j == NJ - 1, [wres, act_r], mark=(j == NJ - 1))
                stt(xg[:, c, 0:n], po[0][:, 0:n], HG(l, s_, c, m_), xg[:, c, 0:n], ALU.mult, ALU.add,
                    [po[1], hg_r, xg_r], [xg_r])

        def qk_stage2(l, hi, n, praw):
            gidx = 0 if hi < 8 else 1
            act_op(stg_sq[0][:, 0:n], praw[0][:, 0:n], AF.Square, [praw[1]], [stg_sq[1]])
            pss = PS[6]
            mm((pss[0][:, 0:n], pss[1]), ones_ap, stg_sq[0][:, 0:n], True, True, [stg_sq[1], small_r])
            act_op(tmpa[:, 0:n], pss[0][:, 0:n], AF.Sqrt, [pss[1]], [tmpa_r], bias=eps_ap, scale=1.0 / 128.0)
            recip(rstd[:, 0:n], tmpa[:, 0:n], [tmpa_r], [rstd_r])
            g_ap = qkg[:, l * 2 + gidx:l * 2 + gidx + 1]
            stt(qn_t[0][:, 0:n], praw[0][:, 0:n], g_ap, rstd[:, 0:n], ALU.mult, ALU.mult,
                [praw[1], qkg_r, rstd_r], [qn_t[1]])

        def qk_stage3(n, is_ctx, dst_ap):
            if is_ctx:
                def prod(ap, r):
                    S.op("dve", lambda e: e.tensor_copy(out=ap, in_=qn_t[0][:, 0:n]), reads=[qn_t[1]], writes=[r])
                store_bf(dst_ap, n, prod)
            else:
                prt = PS[7]
                mm((prt[0][:, 0:n], prt[1]), perm_ap, qn_t[0][:, 0:n], True, True, [qn_t[1], small_r])
                tt("dve", tmpb[:, 0:n], qn_t[0][:, 0:n], rope[:, 0:n], ALU.mult, [qn_t[1], rope_r], [tmpb_r])
                tt("dve", tmpc[:, 0:n], prt[0][:, 0:n], rope[:, NT:NT + n], ALU.mult, [prt[1], rope_r], [tmpc_r])
                def prod(ap, r):
                    tt("dve", ap, tmpb[:, 0:n], tmpc[:, 0:n], ALU.add, [tmpb_r, tmpc_r], [r])
                store_bf(dst_ap, n, prod)

        stg_sq = sb("stg_sq", [128, NT], BF16)
        qn_t = sb("qn_t", [128, NT], BF16)
        uft, uft_r = sb("uft", [128, 4 * NT], BF16)

        def stage_A(l, t0, n, is_ctx, after_mod=None):
            m_ = 1 if is_ctx else 0
            par = l % 2
            ffn(l, 0, n, m_)
            modulate(l, 1, n, m_)
            if after_mod is not None:
                after_mod()
            if not is_ctx:
                S.dma("sp", rope[:, :].rearrange("p (a t) -> p a t", a=2)[:, :, 0:n], IN["rope"][:, :, t0:t0 + n], rope_r, writes=[rope_r])
            kv_only = is_ctx and l == L - 1
            heads = list(range(8 if kv_only else 0, 10))
            def qk_dst(hi):
                return QD[par][hi, :, t0:t0 + n] if hi < 8 else KD[par][hi - 8, :, lsl(t0, n, is_ctx)]
            for st in range(len(heads) + 2):
                if st < len(heads):
                    hi = heads[st]
                    wt, wres = load_w(l, "wqk", hi)
                    pp = PS[hi % 4]
                    for kc in range(KC):
                        mm((pp[0][:, 0:n], pp[1]), wt[:, kc * 128:(kc + 1) * 128], hb[:, kc, 0:n], kc == 0, kc == KC - 1,
                           [wres, hb_r], mark=(kc == KC - 1))
                if 0 <= st - 2 < len(heads):
                    qk_stage3(n, is_ctx, qk_dst(heads[st - 2]))
                if 0 <= st - 1 < len(heads):
                    hi1 = heads[st - 1]
                    qk_stage2(l, hi1, n, PS[hi1 % 4])
            wt, wres = load_w(l, "wv", 0)
            for tb in range(n // 128):
                pp = PS[4 + (tb % 2)]
                for kc in range(KC):
                    mm((pp[0][:, 0:256], pp[1]), hb[:, kc, tb * 128:(tb + 1) * 128], wt[:, kc * 256:(kc + 1) * 256],
                       kc == 0, kc == KC - 1, [wres, hb_r], mark=(kc == KC - 1))
                def prod(ap, r, pp=pp):
                    act_op(ap, pp[0][:, 0:256], AF.Copy, [pp[1]], [r])
                store_bf(VD[par][lsl(t0 + tb * 128, 128, is_ctx), :], 256, prod)
            if kv_only:
                return
            for ci in range(8):
                wt, wres = load_w(l, "wfp", ci)
                pp = PS[ci % 4]
                for kc in range(KC):
                    mm((pp[0][:, 0:n], pp[1]), wt[:, kc * 128:(kc + 1) * 128], hb[:, kc, 0:n], kc == 0, kc == KC - 1,
                       [wres, hb_r], mark=(kc == KC - 1))
                if ci < 4:
                    act_op(uft[:, ci * NT:ci * NT + n], pp[0][:, 0:n], AF.Copy, [pp[1]], [uft_r])
                else:
                    S.op("dve", lambda e, pp=pp: e.tensor_copy(out=stf[:, 0:n], in_=pp[0][:, 0:n]), reads=[pp[1]], writes=[stf_r])
                    S.dma("sp", PD[par][ci - 4, :, lsl(t0, n, is_ctx)], stf[:, 0:n], stf_r,
                          reads=[stf_r], store=True)
            for tb in range(n // 128):
                for gi in range(4):
                    pp = PS[4 + ((tb * 4 + gi) % 2)]
                    mm((pp[0][:, 0:256], pp[1]), uft[:, gi * NT + tb * 128:gi * NT + (tb + 1) * 128], csc[:, :], True, True,
                       [uft_r, csc_r])
                    def prod(ap, r, pp=pp):
                        act_op(ap, pp[0][:, 0:256], AF.Copy, [pp[1]], [r])
                    store_bf(UD[par][lsl(t0 + tb * 128, 128, is_ctx), gi * 256:(gi + 1) * 256],
                             256, prod)
            for ci in range(48):
                wt, wres = load_w(l, "wg", ci)
                pp = PS[ci % 4]
                for kc in range(KC):
                    mm((pp[0][:, 0:n], pp[1]), wt[:, kc * 128:(kc + 1) * 128], hb[:, kc, 0:n], kc == 0, kc == KC - 1,
                       [wres, hb_r], mark=(kc == KC - 1))
                def prod(ap, r, pp=pp):
                    act_op(ap, pp[0][:, 0:n], AF.Sigmoid, [pp[1]], [r])
                store_bf(GD[par][ci, :, t0:t0 + n], n, prod)

        def attention(l, t0, n, is_ctx):
            par = l % 2
            S.dma("sp", qgt.rearrange("p (h t) -> p h t", h=8)[:, :, 0:n], QD[par][:, :, t0:t0 + n].rearrange("h p t -> p h t"),
                  qgt_r, writes=[qgt_r])
            ktv = kt.rearrange("p (g t) -> p g t", g=2)
            vtv = vt.rearrange("p (b c) -> p b c", b=8)
            S.dma("sp", ktv[:, :, 768:1024], KD[par][:, :, SEQ:SEQ + CTX].rearrange("g p t -> p g t"), kt_r, writes=[kt_r])
            S.dma("sp", vtv[:, 6:8, :], VD[par][SEQ:SEQ + CTX, :].rearrange("(b k) c -> k b c", b=2), vt_r, writes=[vt_r])
            if not is_ctx:
                if t0 == 0:
                    S.dma("sp", ktv[:, :, 0:128], KD[par][:, :, SEQ - 128:SEQ].rearrange("g p t -> p g t"), kt_r, writes=[kt_r])
                    S.dma("sp", vtv[:, 0:1, :], VD[par][SEQ - 128:SEQ, :].rearrange("(b k) c -> k b c", k=128), vt_r, writes=[vt_r])
                    S.dma("sp", ktv[:, :, 128:768], KD[par][:, :, 0:640].rearrange("g p t -> p g t"), kt_r, writes=[kt_r])
                    S.dma("sp", vtv[:, 1:6, :], VD[par][0:640, :].rearrange("(b k) c -> k b c", k=128), vt_r, writes=[vt_r])
                else:
                    S.dma("sp", ktv[:, :, 0:768], KD[par][:, :, t0 - 128:t0 + 640].rearrange("g p t -> p g t"), kt_r, writes=[kt_r])
                    S.dma("sp", vtv[:, 0:6, :], VD[par][t0 - 128:t0 + 640, :].rearrange("(b k) c -> k b c", k=128), vt_r, writes=[vt_r])
            qv = qgt.rearrange("p (h t) -> p h t", h=8)
            atv = at.rearrange("p (h t) -> p h t", h=8)
            it = 0
            for qb in range(n // 128):
                for g2 in range(2):
                    blocks = []
                    if not is_ctx:
                        first_q = (t0 == 0 and qb == 0)
                        last_q = (t0 + n == HALF and qb == n // 128 - 1)
                        mL = (emask, emask_r, 0) if first_q else (masks, masks_r, 0)
                        mR = (emask, emask_r, 1) if last_q else (masks, masks_r, 1)
                        blocks = [(qb, mL), (qb + 1, None), (qb + 2, mR)]
                        if False:
                            blocks = blocks[1:]
                    blocks += [(6, None), (7, None)]
                    po = PS[2 + 2 * (it % 2)]
                    pz = PS[3 + 2 * (it % 2)]
                    rhs_q = qv[:, 4 * g2:4 * g2 + 4, qb * 128:(qb + 1) * 128]
                    for bi, (slot, mi) in enumerate(blocks):
                        pst = PS[bi % 2]
                        mm((pst[0][:, :].rearrange("p (h t) -> p h t", h=4), pst[1]), ktv[:, g2, slot * 128:(slot + 1) * 128], rhs_q,
                           True, mi is None, [kt_r, qgt_r])
                        if mi is not None:
                            mm((pst[0][:, :], pst[1]), ident_ap, mi[0][:, mi[2] * 512:(mi[2] + 1) * 512], False, True, [small_r, mi[1]])
                        eb, eb_r = ebuf[bi % 2]
                        act_op(eb, pst[0][:, :], AF.Exp, [pst[1]], [eb_r], scale=att_scale)
                        first = bi == 0
                        last = bi == len(blocks) - 1
                        mm((po[0][:, :], po[1]), vtv[:, slot, g2 * 128:(g2 + 1) * 128], eb, first, last, [vt_r, eb_r], mark=True)
                        mm((pz[0][:, :], pz[1]), ones_ap, eb, first, last, [small_r, eb_r], mark=True)
                    for h4 in range(4):
                        hh = 4 * g2 + h4
                        ts("dve", rden[:, h4 * 128:(h4 + 1) * 128], pz[0][:, h4 * 128:(h4 + 1) * 128],
                           sexp[:, l * 8 + hh:l * 8 + hh + 1], None, ALU.add, None, [pz[1], sexp_r], [rden_r])
                    recip(rden[:, :], rden[:, :], [rden_r], [rden_r])
                    tt("dve", atv[:, 4 * g2:4 * g2 + 4, qb * 128:(qb + 1) * 128], po[0][:, :].rearrange("p (h t) -> p h t", h=4),
                       rden[:, :].rearrange("p (h t) -> p h t", h=4), ALU.mult, [po[1], rden_r], [at_r])
                    it += 1

        def fourier(l, t0, n, is_ctx):
            par = l % 2
            if is_ctx:
                nsc, sbase, rstride, cbase = CTX // 128, SEQ, SEQ // CTX, 0
            else:
                nsc, sbase, rstride, cbase = SEQ // 128, 0, 1, None
            k = 0
            for sc in range(nsc):
                ut, ur = ucs[k % 3]
                tb_, tr = tabt[k % 3]
                k += 1
                S.dma("sp", ut, UD[par][sbase + sc * 128:sbase + (sc + 1) * 128, :], ur, writes=[ur])
                tv = tb_.rearrange("p (a t) -> p a t", a=2)
                if rstride == 1:
                    src = IN["tab"][:, sc * 128:(sc + 1) * 128, t0:t0 + n].rearrange("a s t -> s a t")
                else:
                    src = IN["ctab"][:, sc * 128:(sc + 1) * 128, 0:n].rearrange("a s t -> s a t")
                S.dma("sp", tv[:, :, 0:n], src, tr, writes=[tr])
                for c4 in range(4):
                    for cs in range(2):
                        first = sc == 0 and cs == 0
                        last = sc == nsc - 1 and cs == 1
                        mm((PS[c4][0][:, 0:n], PS[c4][1]), ut[:, (c4 * 2 + cs) * 128:(c4 * 2 + cs + 1) * 128], tv[:, cs, 0:n],
                           first, last, [ur, tr], mark=(last or (c4 == 3 and cs == 1)))
            sc_f = float(np.sqrt(SEQ / CTX)) if is_ctx else 1.0
            for c4 in range(4):
                act_op(yt[:, c4 * NT:c4 * NT + n], PS[c4][0][:, 0:n], AF.Copy, [PS[c4][1]], [yt_r], scale=sc_f)

        def pool_mix(l, t0, n, is_ctx):
            par = l % 2
            W = n + 16
            uv = upt.rearrange("p (g t) -> p g t", g=4)
            if is_ctx:
                S.op("dve", lambda e: e.memset(upt[:, :], 0.0), writes=[upt_r])
                S.dma("sp", uv[:, :, 8:8 + n], PD[par][:, :, SEQ:SEQ + CTX].rearrange("g p t -> p g t"), upt_r, writes=[upt_r])
            elif t0 == 0:
                S.dma("sp", uv[:, :, 0:8], PD[par][:, :, SEQ - 8:SEQ].rearrange("g p t -> p g t"), upt_r, writes=[upt_r])
                S.dma("sp", uv[:, :, 8:W], PD[par][:, :, 0:n + 8].rearrange("g p t -> p g t"), upt_r, writes=[upt_r])
            else:
                S.dma("sp", uv[:, :, 0:W], PD[par][:, :, t0 - 8:t0 + n + 8].rearrange("g p t -> p g t"), upt_r, writes=[upt_r])
            if not is_ctx and t0 == 0:
                for gq in range(4):
                    ts("dve", uv[:, gq, 0:8], uv[:, gq, 0:8], eflag[:, 0:1], None, ALU.mult, None, [upt_r, eflag_r], [upt_r])
            if not is_ctx and t0 + n == HALF:
                for gq in range(4):
                    ts("dve", uv[:, gq, 8 + n:W], uv[:, gq, 8 + n:W], eflag[:, 1:2], None, ALU.mult, None, [upt_r, eflag_r], [upt_r])
            rsl = lsl(t0, n, is_ctx)
            S.dma("sp", rct.rearrange("p (g t) -> p g t", g=4)[:, :, 0:n],
                  IN["rc"][:, :, (HALF if is_ctx else t0):(HALF if is_ctx else t0) + n], rct_r, writes=[rct_r])
            rv = rct.rearrange("p (g t) -> p g t", g=4)
            for gi, w in enumerate((2, 4, 8, 16)):
                u = uv[:, gi, :]
                cur, cur_r, ln_ = u, upt_r, W
                step = 1
                bufs = [(pa, pa_r), (pb, pb_r)]
                bi = 0
                while step < w:
                    o, o_r = bufs[bi % 2]
                    bi += 1
                    nl = ln_ - step
                    tt("dve", o[:, 0:nl], cur[:, 0:nl], cur[:, step:step + nl], ALU.add, [cur_r], [o_r])
                    cur, cur_r, ln_ = o, o_r, nl
                    step *= 2
                st_ = 8 - w // 2
                o, o_r = bufs[bi % 2]
                tt("dve", o[:, 0:n], cur[:, st_:st_ + n], rv[:, gi, 0:n], ALU.mult, [cur_r, rct_r], [o_r])
                tt("dve", pt[:, gi * NT:gi * NT + n], o[:, 0:n], u[:, 8:8 + n], ALU.subtract, [o_r, upt_r], [pt_r])

        def merge_out(l, t0, n, is_ctx):
            par = l % 2
            m_ = 1 if is_ctx else 0
            atv = at.rearrange("p (h t) -> p h t", h=8)
            for c in range(KC):
                wt, wres = load_w(l, "wmg", c)
                gt, gr = gat[c % 2]
                gv = gt.rearrange("p (a t) -> p a t", a=3)
                S.dma("sp", gv[:, :, 0:n], GD[par][:, :, t0:t0 + n].rearrange("(a c) p t -> c p a t", a=3)[c], gr, writes=[gr])
                o3 = 0 if c % 2 == 0 else 4
                pA, pF, pP = PS[o3], PS[o3 + 1], PS[o3 + 2]
                for kc in range(8):
                    mm((pA[0][:, 0:n], pA[1]), wt[:, kc * 128:(kc + 1) * 128], atv[:, kc, 0:n], kc == 0, kc == 7, [wres, at_r], mark=(kc == 7))
                for kc in range(4):
                    mm((pF[0][:, 0:n], pF[1]), wt[:, (8 + kc) * 128:(9 + kc) * 128], yt[:, kc * NT:kc * NT + n], kc == 0, kc == 3,
                       [wres, yt_r], mark=(kc == 3))
                gi = c // 4
                mm((pP[0][:, 0:n], pP[1]), wt[:, 12 * 128:13 * 128], pt[:, gi * NT:gi * NT + n], True, True, [wres, pt_r])
                tt("dve", tmpa[:, 0:n], pA[0][:, 0:n], gv[:, 0, 0:n], ALU.mult, [pA[1], gr], [tmpa_r])
                tt("dve", tmpb[:, 0:n], pF[0][:, 0:n], gv[:, 1, 0:n], ALU.mult, [pF[1], gr], [tmpb_r])
                stt(tmpc[:, 0:n], pP[0][:, 0:n], pscale[:, l * KC + c:l * KC + c + 1], gv[:, 2, 0:n], ALU.mult, ALU.mult,
                    [pP[1], pscale_r, gr], [tmpc_r])
                tt("dve", tmpa[:, 0:n], tmpa[:, 0:n], tmpb[:, 0:n], ALU.add, [tmpa_r, tmpb_r], [tmpa_r])
                tt("dve", hb[:, c, 0:n], tmpa[:, 0:n], tmpc[:, 0:n], ALU.add, [tmpa_r, tmpc_r], [hb_r])
            for c in range(KC):
                wt, wres = load_w(l, "wout", c)
                po = PS[3 if c % 2 == 0 else 7]
                for kc in range(KC):
                    mm((po[0][:, 0:n], po[1]), wt[:, kc * 128:(kc + 1) * 128], hb[:, kc, 0:n], kc == 0, kc == KC - 1,
                       [wres, hb_r], mark=(kc == KC - 1))
                stt(xg[:, c, 0:n], po[0][:, 0:n], MOD(l, 5, c, m_), xg[:, c, 0:n], ALU.mult, ALU.add, [po[1], mod_r, xg_r], [xg_r])

        out_toks = []
        for sp_ in range(L + 1):
            if 1 <= sp_ + 1 < L:
                for i_ in range(2):
                    precast(sp_ + 1, i_)
            act_groups = [g_ for g_ in groups if (sp_ >= 1 and not (g_[2] and sp_ - 1 == L - 1)) or sp_ < L]
            preloaded = [False]
            for gi_, (t0, n, is_ctx) in enumerate(act_groups):
                do_bc = sp_ >= 1 and not (is_ctx and sp_ - 1 == L - 1)
                do_a = sp_ < L
                src = IN["xin"] if sp_ == 0 else XD
                if not preloaded[0]:
                    S.dma("sp", xg[:, :, 0:n], src[:, :, t0:t0 + n].rearrange("k p t -> p k t"), xg_r, writes=[xg_r])
                preloaded[0] = False
                def after_mod(gi_=gi_, t0=t0, n=n, src=src):
                    S.dma("sp", XD[:, :, t0:t0 + n].rearrange("k p t -> p k t"), xg[:, :, 0:n], xg_r, reads=[xg_r], store=True)
                    if gi_ + 1 < len(act_groups):
                        t1, n1, _ = act_groups[gi_ + 1]
                        S.dma("sp", xg[:, :, 0:n1], src[:, :, t1:t1 + n1].rearrange("k p t -> p k t"), xg_r, writes=[xg_r])
                        preloaded[0] = True
                if do_bc:
                    lb = sp_ - 1
                    attention(lb, t0, n, is_ctx)
                    fourier(lb, t0, n, is_ctx)
                    pool_mix(lb, t0, n, is_ctx)
                    merge_out(lb, t0, n, is_ctx)
                    ffn(lb, 1, n, 1 if is_ctx else 0)
                if do_a:
                    stage_A(sp_, t0, n, is_ctx, after_mod)
                if sp_ == L:
                    tok = S.dma("sp", yout[:, :, t0:t0 + n].rearrange("k p t -> p k t"), xg[:, :, 0:n], xg_r, reads=[xg_r], store=True)
                    out_toks.append(tok)
            while pc_queue:
                precast(*pc_queue.pop(0))
            if sp_ < L:
                xbarrier(par=sp_ % 2)
        S.final_wait("sp", out_toks[-1:])

        with nc.Block() as block:
            S.emit(block)
    return nc


_SKIP = ("rope", "small", "masks", "csc", "tab", "rc")


def run(inputs, cfg):
    maps = _prep_inputs(inputs, cfg)
    shapes = {k: (v.shape, v.dtype) for k, v in maps[0].items()}
    nc = build_program(cfg, shapes)
    res = run_bass_kernel_spmd(nc, maps, core_ids=list(range(NCORES)))
    yT = np.concatenate([res.results[c]["yout"] for c in range(NCORES)], 2)
    SEQ = cfg["SEQ"]
    return np.ascontiguousarray(yT.reshape(D, SEQ).T.reshape(1, SEQ, D)).astype(np.float32)


def kernel(**inputs):
    return run(inputs, CFG)
```

```python
import numpy as np
import ml_dtypes
import concourse.bass as bass
import concourse.mybir as mybir
from concourse.bass_utils import run_bass_kernel_spmd

F32 = mybir.dt.float32
BF16 = mybir.dt.bfloat16
AF = mybir.ActivationFunctionType
ALU = mybir.AluOpType

CFG = dict(DEPTH=4, SEQ=8192, D_FF=5632)
D = 2048
KC = 16
CTX = 256
NT = 512
EPS = 1e-6
GRID_W = 64


class Res:
    def __init__(self, name, ap=None, parent=None):
        self.name = name
        self.ap = ap
        self.last_w = None
        self.readers = []
        self.parent = parent
        self.children = []
        self.dma_sem = None
        self.dma_cnt = 0
        if parent is not None:
            parent.children.append(self)


class Sched:
    COMPUTE = ("pe", "act", "dve", "pool")
    ALL = ("pe", "act", "dve", "pool", "sp")

    def __init__(self, nc, es):
        self.nc = nc
        self.es = es
        self.streams = {e: [] for e in self.ALL}
        self.esem = {e: es.enter_context(nc.semaphore("S_" + e)) for e in self.COMPUTE}
        self.tick = {e: 0 for e in self.COMPUTE}
        self.seen = {e: {} for e in self.ALL}
        self.pending_stores = {}
        self.nsem = 0
        self.pid = {}

    def _conflicts(self, r, write):
        toks = []
        def add(res):
            if res.last_w is not None:
                toks.append(res.last_w)
            if write:
                toks.extend(res.readers)
        add(r)
        if r.parent is not None:
            add(r.parent)
        for c in r.children:
            add(c)
        return toks

    def _waits(self, eng, reads, writes):
        need = {}
        for r in reads:
            for t in self._conflicts(r, False):
                self._need(eng, need, t)
        for r in writes:
            for t in self._conflicts(r, True):
                self._need(eng, need, t)
        out = []
        for key, (sem, val) in need.items():
            if self.seen[eng].get(key, 0) < val:
                self.seen[eng][key] = val
                out.append((sem, val))
        return out

    def _need(self, eng, need, tok):
        kind, key, sem, val = tok
        if kind == "eng" and key == eng:
            return
        if kind == "eng":
            assert val <= self.tick[key], "dangling PE mark dependency (would deadlock)"
        k = (kind, key)
        if k not in need or need[k][1] < val:
            need[k] = (sem, val)

    def _record(self, tok, reads, writes):
        for r in reads:
            r.readers.append(tok)
        for r in writes:
            r.last_w = tok
            r.readers = []

    def op(self, eng, fn, reads=(), writes=(), mark=True):
        waits = self._waits(eng, reads, writes)
        if mark:
            self.tick[eng] += 1
            tok = ("eng", eng, self.esem[eng], self.tick[eng])
            inc = (self.esem[eng], 1)
        else:
            tok = ("eng", eng, self.esem[eng], self.tick[eng] + 1)
            inc = None
        self.streams[eng].append((waits, fn, inc))
        self._record(tok, reads, writes)

    def dma(self, q, out, in_, sb, reads=(), writes=(), store=False, **kw):
        if sb.dma_sem is None:
            sb.dma_sem = self.es.enter_context(self.nc.semaphore("D_%d" % self.nsem))
            self.nsem += 1
        waits = self._waits(q, reads, writes)
        sb.dma_cnt += 16
        tok = ("dma", sb.name, sb.dma_sem, sb.dma_cnt)
        def _f(e, out=out, in_=in_, kw=kw):
            pid = self.pid[id(e)]
            o = out(pid) if callable(out) else out
            i = in_(pid) if callable(in_) else in_
            try:
                return e.dma_start(out=o, in_=i, **kw)
            except Exception:
                print("DMA FAIL", o, i, kw)
                raise
        self.streams[q].append((waits, _f, (sb.dma_sem, 16)))
        self._record(tok, reads, writes)
        if store:
            self.pending_stores[sb.name] = tok
        return tok

    def barrier(self):
        for q in ("sp", "pool"):
            waits = []
            for tok in self.pending_stores.values():
                kind, key, sem, val = tok
                k = (kind, key)
                if self.seen[q].get(k, 0) < val:
                    self.seen[q][k] = val
                    waits.append((sem, val))
            if waits:
                self.streams[q].append((waits, None, None))
        self.pending_stores = {}

    def custom(self, q, fn, waits=()):
        self.streams[q].append((list(waits), fn, None))

    def pending_waits(self, q):
        waits = []
        for tok in self.pending_stores.values():
            kind, key, sem, val = tok
            k = (kind, key)
            if self.seen[q].get(k, 0) < val:
                self.seen[q][k] = val
                waits.append((sem, val))
        return waits

    def final_wait(self, q, toks):
        waits = [(t[2], t[3]) for t in toks]
        self.streams[q].append((waits, None, None))

    def emit(self, block):
        nc = self.nc
        def run(stream, need_pid=False):
            def f(e):
                self.pid[id(e)] = e.partition_id() if need_pid else None
                for waits, fn, inc in stream:
                    for sem, val in waits:
                        e.wait_ge(sem, val)
                    if fn is not None:
                        ins = fn(e)
                        if inc is not None:
                            ins.then_inc(*inc)
            return f
        block.tensor(run(self.streams["pe"]))
        block.scalar(run(self.streams["act"], True))
        block.vector(run(self.streams["dve"]))
        block.gpsimd(run(self.streams["pool"], True))
        block.sync(run(self.streams["sp"], True))


def _fm_tiles(w, nchunk):
    K = w.shape[0]
    return np.ascontiguousarray(w.reshape(K // 128, 128, nchunk, 128).transpose(2, 1, 0, 3))


NPIECE = 136
NCORES = 2
PADT = 128


def _wlayout(NJ):
    items = [("gu0", 4096, NJ), ("dn0", NJ * 128, 16), ("gu1", 4096, NJ), ("dn1", NJ * 128, 16),
             ("wqk", 2048, 10), ("wv", 4096, 1), ("wfp", 2048, 8), ("wg", 2048, 48),
             ("wmg", 13 * 128, 16), ("wout", 2048, 16)]
    lay = {}
    off = 0
    for name, f, nt in items:
        lay[name] = (off, f, nt)
        off += 128 * f * nt
    q = 2048 * NPIECE * NCORES * 16
    tot = (off + q - 1) // q * q
    return lay, tot


def _constants(SEQ):
    TA = SEQ + CTX
    rows_n = SEQ // GRID_W
    rows = np.repeat(np.arange(rows_n), GRID_W).astype(np.float32)
    cols = np.tile(np.arange(GRID_W), rows_n).astype(np.float32)
    n_freq = 32
    freqs = (10000.0 ** (-np.arange(n_freq, dtype=np.float32) / n_freq)).astype(np.float32)
    ar = rows[None, :] * freqs[:, None]
    ac = cols[None, :] * freqs[:, None]
    cosT = np.concatenate([np.cos(ar), np.cos(ar), np.cos(ac), np.cos(ac)], 0).astype(np.float32)
    sinT = np.concatenate([-np.sin(ar), np.sin(ar), -np.sin(ac), np.sin(ac)], 0).astype(np.float32)
    rope = np.ascontiguousarray(np.stack([cosT, sinT], 1))
    perm = np.zeros((128, 128), np.float32)
    for d in range(128):
        src = d + 32 if (d % 64) < 32 else d - 32
        perm[src, d] = 1.0
    ident = np.eye(128, dtype=np.float32)
    ones = np.ones((128, 128), np.float32)
    kk = np.arange(128)[:, None]
    qq = np.arange(128)[None, :]
    mleft = np.where(kk >= qq, 0.0, -30000.0).astype(np.float32)
    mright = np.where(kk <= qq, 0.0, -30000.0).astype(np.float32)
    mats = np.stack([perm, ident, ones, np.tile(mleft, (1, 4))[:, :128] * 0], 0)
    small = np.concatenate([perm, ident, ones], 1).astype(ml_dtypes.bfloat16)
    masks = np.concatenate([np.tile(mleft, (1, 4)), np.tile(mright, (1, 4))], 1).astype(ml_dtypes.bfloat16)
    cc = np.arange(128)
    angc = 2 * np.pi * np.outer(cc, cc) / 128.0
    csc = (np.concatenate([np.cos(angc), np.sin(angc)], 1) / np.sqrt(128.0)).astype(ml_dtypes.bfloat16)
    s = np.arange(SEQ, dtype=np.int64)
    idx = np.outer(s, s) % SEQ
    ang = (2 * np.pi / SEQ) * idx
    tab = np.empty((2, SEQ, SEQ), ml_dtypes.bfloat16)
    tab[0] = (np.cos(ang) / np.sqrt(SEQ)).astype(ml_dtypes.bfloat16)
    tab[1] = (-np.sin(ang) / np.sqrt(SEQ)).astype(ml_dtypes.bfloat16)
    del ang, idx
    rc = np.zeros((4, TA), np.float32)
    for gi, w in enumerate((2, 4, 8, 16)):
        for (base, n) in ((0, SEQ), (SEQ, CTX)):
            t = np.arange(n)
            lo = np.clip(t - w // 2, 0, n)
            hi = np.clip(t + (w - w // 2), 0, n)
            rc[gi, base:base + n] = 1.0 / (hi - lo)
    rcb = np.ascontiguousarray(np.broadcast_to(rc[None], (128, 4, TA)))
    return dict(rope=rope, small=small, masks=masks, csc=csc, tab=tab, rc=rcb)


def _prep_inputs(inp, cfg):
    L, SEQ, DFF = cfg["DEPTH"], cfg["SEQ"], cfg["D_FF"]
    NJ = DFF // 128
    f = lambda a: np.asarray(a, dtype=np.float32)
    x = f(inp["x"])[0]
    ctx = f(inp["ctx"])[0]
    xa = np.concatenate([x, ctx], 0)
    m = {}
    m["xin"] = np.ascontiguousarray(xa.T.reshape(KC, 128, SEQ + CTX))
    cc = np.stack([f(inp["c"])[0], f(inp["c_ctx"])], -1)
    m["cin"] = np.ascontiguousarray(cc.reshape(KC, 128, 2).transpose(1, 0, 2))
    w_ada = f(inp["w_ada"])
    m["wada"] = np.stack([_fm_tiles(w_ada[l], 144) for l in range(L)], 0)
    m["bada"] = np.ascontiguousarray(f(inp["b_ada"])[:L].reshape(L, 144, 128).transpose(2, 0, 1))
    m["normw"] = np.ascontiguousarray(f(inp["norm_w"])[:L].reshape(L, 3, KC, 128).transpose(3, 0, 1, 2))
    g = f(inp["ffn_w_gate"]); u = f(inp["ffn_w_up"]); dn = f(inp["ffn_w_down"])
    gu = np.empty((L, 2, NJ, 128, 2, KC, 128), np.float32)
    dd = np.empty((L, 2, KC, 128, NJ, 128), np.float32)
    for l in range(L):
        for s in range(2):
            gu[l, s, :, :, 0] = _fm_tiles(g[l, s], NJ)
            gu[l, s, :, :, 1] = _fm_tiles(u[l, s], NJ)
            dd[l, s] = dn[l, s].reshape(NJ, 128, KC, 128).transpose(2, 1, 0, 3)
    m["wgu"] = gu
    m["wdn"] = dd
    w_in = f(inp["w_in"])
    m["wqk"] = np.stack([_fm_tiles(w_in[l][:, 0:1280], 10) for l in range(L)], 0)
    m["wv"] = np.stack([np.ascontiguousarray(w_in[l][:, 1280:1536].reshape(KC, 128, 256).transpose(1, 0, 2)) for l in range(L)], 0)
    m["wfp"] = np.stack([_fm_tiles(w_in[l][:, 1536:2560], 8) for l in range(L)], 0)
    m["wg"] = np.stack([_fm_tiles(w_in[l][:, 2560:8704], 48) for l in range(L)], 0)
    wao = f(inp["w_attn_o"]); wfo = f(inp["w_fourier"]); wpl = f(inp["w_pool"])
    mm = np.empty((L, KC, 128, 13, 128), np.float32)
    for l in range(L):
        mm[l, :, :, 0:8] = _fm_tiles(wao[l], 16)
        mm[l, :, :, 8:12] = _fm_tiles(wfo[l], 16)
        mm[l, :, :, 12] = wpl[l].reshape(4, 128, 4, 128).transpose(0, 2, 1, 3).reshape(16, 128, 128)
    m["wmg"] = mm
    m["wout"] = np.stack([_fm_tiles(f(inp["w_out"])[l], 16) for l in range(L)], 0)
    qg = f(inp["q_gain"])[:L]; kg = f(inp["k_gain"])[:L]
    m["qkg"] = np.ascontiguousarray(np.stack([qg, kg], -1).transpose(1, 0, 2))
    m["sink"] = np.ascontiguousarray(np.broadcast_to(f(inp["sink"])[:L][None], (128, L, 8)))
    m["pscale"] = np.ascontiguousarray(f(inp["pool_scale"])[:L].reshape(L, KC, 128).transpose(2, 0, 1))
    lay, tot = _wlayout(NJ)
    blob = np.zeros((L, tot), np.float32)
    for l in range(L):
        parts = [gu[l, 0], dd[l, 0], gu[l, 1], dd[l, 1], m["wqk"][l], m["wv"][l], m["wfp"][l], m["wg"][l],
                 m["wmg"][l], m["wout"][l]]
        o = 0
        for p_ in parts:
            blob[l, o:o + p_.size] = p_.ravel()
            o += p_.size
    for k in ("wgu", "wdn", "wqk", "wv", "wfp", "wg", "wmg", "wout"):
        del m[k]
    m.update(_constants(SEQ))
    HALF = SEQ // NCORES
    HW = tot // NCORES
    kk = np.arange(128)[:, None]
    qq = np.arange(128)[None, :]
    mleft = np.tile(np.where(kk >= qq, 0.0, -30000.0), (1, 4)).astype(np.float32)
    mright = np.tile(np.where(kk <= qq, 0.0, -30000.0), (1, 4)).astype(np.float32)
    dead = np.full((128, 512), -30000.0, np.float32)
    nonce = np.array([[int(np.random.randint(1, 2 ** 30)), 0, 0, 0]], np.int32)
    xin = m.pop("xin"); wada = m.pop("wada"); bada = m.pop("bada")
    maps = []
    for c in range(NCORES):
        mc = dict(m)
        mc["xin"] = np.ascontiguousarray(np.concatenate([xin[:, :, c * HALF:(c + 1) * HALF], xin[:, :, SEQ:]], 2))
        mc["wada"] = np.ascontiguousarray(wada[:, c * 72:(c + 1) * 72])
        mc["bada"] = np.ascontiguousarray(bada[:, :, c * 72:(c + 1) * 72])
        mc["wblob"] = np.ascontiguousarray(blob[:, c * HW:(c + 1) * HW])
        eL = dead if c == 0 else mleft
        eR = dead if c == NCORES - 1 else mright
        mc["emask"] = np.concatenate([eL, eR], 1).astype(ml_dtypes.bfloat16)
        mc["eflag"] = np.ascontiguousarray(np.broadcast_to(
            np.array([[0.0 if c == 0 else 1.0, 0.0 if c == NCORES - 1 else 1.0]], np.float32), (128, 2)))
        mc["rope"] = np.ascontiguousarray(m["rope"][:, :, c * HALF:(c + 1) * HALF])
        mc["rc"] = np.ascontiguousarray(np.concatenate([m["rc"][:, :, c * HALF:(c + 1) * HALF], m["rc"][:, :, SEQ:]], 2))
        sg = (np.arange(SEQ) + c * HALF) % SEQ
        mc["tab"] = np.ascontiguousarray(m["tab"][:, sg, c * HALF:(c + 1) * HALF])
        mc["ctab"] = np.ascontiguousarray(m["tab"][:, ::SEQ // CTX, 0:CTX][:, 0:CTX])
        mc["nonce"] = nonce
        maps.append(mc)
    return maps


def build_program(cfg, shapes):
    from contextlib import ExitStack
    L, SEQ, DFF = cfg["DEPTH"], cfg["SEQ"], cfg["D_FF"]
    NJ = DFF // 128
    TA = SEQ + CTX
    NXG = SEQ // NT
    HALF = SEQ // NCORES
    NXGL = NXG // NCORES
    assert NXG % NCORES == 0
    TL = HALF + CTX
    TP = SEQ + CTX + 3 * PADT
    CTXP = SEQ + 2 * PADT
    groups = [(g * NT, NT, False) for g in range(NXGL)] + [(HALF, CTX, True)]
    NB = SEQ // 128

    def lsl(t0l, n, is_ctx):
        st = SEQ + (t0l - HALF) if is_ctx else t0l
        return slice(st, st + n)
    att_scale = 128.0 ** -0.5

    nc = bass.Bass("TRN2", target_bir_lowering=False)
    dt_of = {np.dtype(np.float32): F32, np.dtype(ml_dtypes.bfloat16): BF16, np.dtype(np.int32): mybir.dt.int32}
    IN = {k: nc.dram_tensor(k, list(shp), dt_of[np.dtype(dtp)], kind="ExternalInput").ap()
          for k, (shp, dtp) in shapes.items()}
    yout = nc.dram_tensor("yout", [KC, 128, HALF], F32, kind="ExternalOutput").ap()

    def dscr(name, shape, dt):
        return nc.dram_tensor(name, shape, dt, kind="Internal").ap()
    def dshr(name, shape, dt):
        return nc.dram_tensor(name, shape, dt, kind="Internal", addr_space="Shared").ap()
    XD = dscr("XD", [KC, 128, TL], F32)
    QD = [dscr("QD%d" % i, [8, 128, TL], BF16) for i in range(2)]
    GD = [dscr("GD%d" % i, [48, 128, TL], BF16) for i in range(2)]
    KD = [dscr("KD%d" % i, [2, 128, TA], BF16) for i in range(2)]
    VD = [dscr("VD%d" % i, [TA, 256], BF16) for i in range(2)]
    UD = [dscr("UD%d" % i, [TA, 1024], BF16) for i in range(2)]
    PD = [dscr("PD%d" % i, [4, 128, TA], F32) for i in range(2)]
    SHK = [dshr("SHK%d" % i, [NCORES, 2, 128, HALF], BF16) for i in range(2)]
    SHV = [dshr("SHV%d" % i, [NCORES, HALF, 256], BF16) for i in range(2)]
    SHU = [dshr("SHU%d" % i, [NCORES, HALF, 1024], BF16) for i in range(2)]
    SHP = [dshr("SHP%d" % i, [NCORES, 4, 128, HALF], F32) for i in range(2)]
    LAY, WTOT = _wlayout(NJ)
    HW = WTOT // NCORES
    WB2 = [dshr("WB%d" % i, [NCORES, HW // 2048, 2048], BF16) for i in range(L)]
    WB = [w.rearrange("n r c -> (n r c)") for w in WB2]
    MODS = dshr("MODS", [NCORES, 128, L * 144], F32)
    FLAGS = dshr("FLAGS", [NCORES, 16], mybir.dt.int32)
    DUMMY = dscr("DUMMY", [1, 4], mybir.dt.int32)

    es = ExitStack()
    with es:
        S = Sched(nc, es)
        def sb(name, shape, dt, parent=None):
            t = es.enter_context(nc.sbuf_tensor("s_" + name, shape, dt))
            return t, Res(name, parent=parent)
        def ps(name):
            t = es.enter_context(nc.psum_tensor(name, [128, 512], F32))
            return t, Res(name)

        PS = [ps("ps%d" % i) for i in range(8)]
        xg, xg_r = sb("xg", [128, KC, NT], F32)
        hb, hb_r = sb("hb", [128, KC, NT], BF16)
        NJA = max(NJ, 61)
        actb, act_r = sb("actb", [128, NJA * NT], BF16)
        sqb, sq_r = sb("sqb", [128, KC * NT], BF16)
        WSL = 3
        WBYTES = max(NJ * 128, 4096)
        wr = [sb("wr%d" % i, [128, WBYTES], BF16) for i in range(WSL)]
        wr_i = [0]
        modt, mod_r = sb("modt", [128, L * 9 * KC * 2], F32)
        amul, amul_r = sb("amul", [128, L * 3 * KC * 2], F32)
        hgt, hg_r = sb("hgt", [128, L * 2 * KC * 2], F32)
        normw, normw_r = sb("normw", [128, L * 3 * KC], F32)
        bada, bada_r = sb("bada", [128, L * 72], F32)
        modh, modh_r = sb("modh", [128, L * 144], F32)
        emask, emask_r = sb("emask", [128, 1024], BF16)
        eflag, eflag_r = sb("eflag", [128, 2], F32)
        nzt, nzt_r = sb("nzt", [1, 4], mybir.dt.int32)
        zt, zt_r = sb("zt", [128, 256], F32)
        cin, cin_r = sb("cin", [128, KC * 2], F32)
        cond, cond_r = sb("cond", [128, KC * 2], F32)
        qkg, qkg_r = sb("qkg", [128, L * 2], F32)
        sinkt, sink_r = sb("sinkt", [128, L * 8], F32)
        pscale, pscale_r = sb("pscale", [128, L * KC], F32)
        small, small_r = sb("small", [128, 384], BF16)
        masks, masks_r = sb("masks", [128, 1024], BF16)
        csc, csc_r = sb("csc", [128, 256], BF16)
        rope, rope_r = sb("rope", [128, 2 * NT], F32)
        rstd, rstd_r = sb("rstd", [128, NT], F32)
        tmpa, tmpa_r = sb("tmpa", [128, NT], F32)
        tmpb, tmpb_r = sb("tmpb", [128, NT], F32)
        tmpc, tmpc_r = sb("tmpc", [128, NT], F32)
        silu_t, silu_r = sb("silu_t", [128, NT], F32)
        stg = [sb("stg%d" % i, [128, NT], BF16) for i in range(3)]
        stg_i = [0]
        stf, stf_r = sb("stf", [128, NT], F32)

        perm_ap = small[:, 0:128]
        ident_ap = small[:, 128:256]
        ones_ap = small[:, 256:384]

        off = [0]
        def carve(name, n, dt=BF16):
            nb = n * (2 if dt == F32 else 1)
            assert off[0] + nb <= NJA * NT, "act scratch overflow"
            v = actb[:, off[0]:off[0] + nb]
            if dt == F32:
                v = v.bitcast(F32)
            off[0] += nb
            return v, Res(name, parent=act_r)
        qgt, qgt_r = carve("qgt", 8 * NT)
        kt, kt_r = carve("kt", 2 * 8 * 128)
        vt, vt_r = carve("vt", 8 * 256)
        rden, rden_r = carve("rden", NT, F32)
        at, at_r = carve("at", 8 * NT)
        yt, yt_r = carve("yt", 4 * NT)
        pt, pt_r = carve("pt", 4 * NT)
        upt, upt_r = carve("upt", 4 * (NT + 16), F32)
        pa, pa_r = carve("pa", NT + 16, F32)
        pb, pb_r = carve("pb", NT + 16, F32)
        rct, rct_r = carve("rct", 4 * NT, F32)
        gat = [carve("gat%d" % i, 3 * NT) for i in range(2)]
        soff = [0]
        def carve_s(name, n):
            assert soff[0] + n <= KC * NT
            v = sqb[:, soff[0]:soff[0] + n]
            soff[0] += n
            return v, Res(name, parent=sq_r)
        ucs = [carve_s("ucs%d" % i, 1024) for i in range(3)]
        tabt = [carve_s("tab%d" % i, 1024) for i in range(3)]
        ebuf = [carve_s("e%d" % i, NT) for i in range(2)]
        sexp, sexp_r = sb("sexp", [128, L * 8], F32)

        def next_w():
            i = wr_i[0] % WSL
            wr_i[0] += 1
            return wr[i]

        pc_res = [Res("pc%d" % l) for l in range(L)]
        PIECE = HW // 2

        def precast(l, i):
            a, b = i * PIECE, (i + 1) * PIECE
            S.dma("pool", lambda pid: WB2[l][bass.ds(pid, 1), a // 2048:a // 2048 + PIECE // 2048, :].rearrange("o r c -> (o r) c"),
                  IN["wblob"][l, a:b].rearrange("(r c) -> r c", c=2048),
                  pc_res[l], writes=[pc_res[l]], store=True, max_dma_last_dim=8192)

        def load_w(l, name, idx):
            base, f, nt = LAY[name]
            assert idx < nt
            off_ = base + idx * 128 * f
            t, r = next_w()
            S.dma("pool", t[:, 0:f], WB[l][off_:off_ + 128 * f].rearrange("(p f) -> p f", f=f), r,
                  reads=[pc_res[l]], writes=[r])
            lw_cnt[0] += 1
            if False:
                precast(*pc_queue.pop(0))
            return t, r

        pc_queue = []
        lw_cnt = [0]

        def mm(out_ps, lhsT, rhs, start, stop, reads, mark=None):
            pt_, pr_ = out_ps
            if mark is None:
                mark = stop
            S.op("pe", lambda e, o=pt_, l=lhsT, r=rhs, st=start, sp=stop: e.matmul(o, lhsT=l, rhs=r, start=st, stop=sp),
                 reads=reads, writes=[pr_], mark=mark)

        def act_op(out, in_, func, reads, writes, bias=None, scale=None):
            kw = {}
            if bias is not None:
                kw["bias"] = bias
            if scale is not None:
                kw["scale"] = scale
            S.op("act", lambda e: e.activation(out=out, in_=in_, func=func, **kw), reads=reads, writes=writes)

        def tt(eng, out, a, b, op, reads, writes):
            S.op(eng, lambda e: e.tensor_tensor(out=out, in0=a, in1=b, op=op), reads=reads, writes=writes)

        def stt(out, a, sc, b, op0, op1, reads, writes):
            S.op("dve", lambda e: e.scalar_tensor_tensor(out=out, in0=a, scalar=sc, in1=b, op0=op0, op1=op1),
                 reads=reads, writes=writes)

        def ts(eng, out, a, s1, s2, op0, op1, reads, writes):
            if op1 is None:
                S.op(eng, lambda e: e.tensor_scalar(out=out, in0=a, scalar1=s1, scalar2=None, op0=op0), reads=reads, writes=writes)
            else:
                S.op(eng, lambda e: e.tensor_scalar(out=out, in0=a, scalar1=s1, scalar2=s2, op0=op0, op1=op1), reads=reads, writes=writes)

        def recip(out, in_, reads, writes):
            S.op("dve", lambda e: e.reciprocal(out=out, in_=in_), reads=reads, writes=writes)

        def store_bf(dst_ap, n, producer):
            t, r = stg[stg_i[0] % 3]
            stg_i[0] += 1
            producer(t[:, 0:n], r)
            S.dma("sp", dst_ap, t[:, 0:n], r, reads=[r], store=True)

        for i_ in range(2):
            precast(0, i_)
        def load_const(t, r, src):
            S.dma("sp", t, src, r, writes=[r])
        load_const(small[:, :], small_r, IN["small"])
        load_const(masks[:, :], masks_r, IN["masks"])
        load_const(emask[:, :], emask_r, IN["emask"])
        load_const(eflag[:, :], eflag_r, IN["eflag"])
        nz_tok = S.dma("sp", nzt[:, :], IN["nonce"], nzt_r, writes=[nzt_r])
        load_const(csc[:, :], csc_r, IN["csc"])
        load_const(normw[:, :], normw_r, IN["normw"].rearrange("p l s c -> p (l s c)"))
        load_const(bada[:, :], bada_r, IN["bada"].rearrange("p l c -> p (l c)"))
        load_const(cin[:, :], cin_r, IN["cin"].rearrange("p k m -> p (k m)"))
        load_const(qkg[:, :], qkg_r, IN["qkg"].rearrange("p l t -> p (l t)"))
        load_const(sinkt[:, :], sink_r, IN["sink"].rearrange("p l h -> p (l h)"))
        load_const(pscale[:, :], pscale_r, IN["pscale"].rearrange("p l c -> p (l c)"))
        act_op(cond[:, :], cin[:, :], AF.Silu, [cin_r], [cond_r])
        act_op(sexp[:, :], sinkt[:, :], AF.Exp, [sink_r], [sexp_r])

        WA_CH = 2
        xgf = xg[:, :, :].rearrange("p k t -> p (k t)")
        half = [(xgf[:, 0:4096], Res("xgA", parent=xg_r)), (xgf[:, 4096:8192], Res("xgB", parent=xg_r))]
        it_ = 0
        for l in range(L):
            pst = PS[l % 2]
            for c0 in range(0, 72, WA_CH):
                wv, wv_r = half[it_ % 2]
                it_ += 1
                S.dma("sp", wv.rearrange("p (a f) -> p a f", a=WA_CH),
                      IN["wada"][l, c0:c0 + WA_CH].rearrange("a p k c -> p a (k c)"), wv_r, writes=[wv_r])
                for a in range(WA_CH):
                    cc_ = c0 + a
                    for kc in range(KC):
                        o = pst[0][:, cc_ * 2:cc_ * 2 + 2]
                        lh = wv[:, a * 2048 + kc * 128: a * 2048 + (kc + 1) * 128]
                        rh = cond[:, kc * 2:(kc + 1) * 2]
                        mm((o, pst[1]), lh, rh, kc == 0, kc == KC - 1, [wv_r, cond_r], mark=(kc == KC - 1))
            for m_ in range(2):
                src = pst[0][:, 0:144].rearrange("p (c m) -> p c m", m=2)[:, :, m_]
                dst = modh[:, l * 144:(l + 1) * 144].rearrange("p (c m) -> p c m", m=2)[:, :, m_]
                tt("dve", dst, src, bada[:, l * 72:(l + 1) * 72], ALU.add, [pst[1], bada_r], [modh_r])
        S.dma("sp", lambda pid: MODS[bass.ds(pid, 1), :, :].rearrange("o p x -> p (o x)"), modh[:, :], modh_r, reads=[modh_r], store=True)
        fsem = es.enter_context(nc.semaphore("fsem"))
        xsem = es.enter_context(nc.semaphore("xsem"))
        pubsem = es.enter_context(nc.semaphore("pubsem"))
        fetsem = es.enter_context(nc.semaphore("fetsem"))
        xb_cnt = [0]
        npub = [0]
        nfet = [0]
        def xbarrier(par=None):
            k = xb_cnt[0]
            xb_cnt[0] += 1
            if par is not None:
                waits = S.pending_waits("act")
                S.pending_stores = {}
                def pub(e, par=par):
                    pid = S.pid[id(e)]
                    e.dma_start(out=SHK[par][bass.ds(pid, 1)].rearrange("o g p t -> (o g) p t"), in_=KD[par][:, :, 0:HALF]).then_inc(pubsem, 16)
                    e.dma_start(out=SHV[par][bass.ds(pid, 1)].rearrange("o t c -> (o t) c"), in_=VD[par][0:HALF, :]).then_inc(pubsem, 16)
                    e.dma_start(out=SHU[par][bass.ds(pid, 1)].rearrange("o t c -> (o t) c"), in_=UD[par][0:HALF, :]).then_inc(pubsem, 16)
                    e.dma_start(out=SHP[par][bass.ds(pid, 1)].rearrange("o g p t -> (o g) p t"), in_=PD[par][:, :, 0:HALF]).then_inc(pubsem, 16)
                    return None
                S.custom("act", pub, waits)
                npub[0] += 4
                waits = [(pubsem, 16 * npub[0])]
            else:
                waits = S.pending_waits("sp")
                S.pending_stores = {}
            if k == 0:
                waits.append((nz_tok[2], nz_tok[3]))
            def fn(e, k=k, par=par):
                pid = S.pid[id(e)]
                e.dma_start(out=FLAGS[bass.ds(pid, 1), k:k + 1], in_=nzt[0:1, 0:1]).then_inc(fsem, 16)
                e.wait_ge(fsem, 16 * (k + 1))
                with e.register("nz%d" % k) as nz, e.register("fa%d" % k) as fa, e.register("fb%d" % k) as fb, \
                        e.register("df%d" % k) as df:
                    e.reg_load(nz, IN["nonce"][0:1, 0:1])
                    e.reg_mov(df, 1)
                    with e.While(df):
                        e.reg_load(fa, FLAGS[0:1, k:k + 1])
                        e.reg_load(fb, FLAGS[1:2, k:k + 1])
                        e.reg_sub(fa, fa, nz)
                        e.reg_sub(fb, fb, nz)
                        e.reg_alu(df, fa, fb, ALU.bitwise_or)
                e.dma_start(out=DUMMY[0:1, 0:1], in_=nzt[0:1, 0:1]).then_inc(xsem, 16)
                if par is not None:
                    oth = (pid + 1) % NCORES
                    e.dma_start(out=KD[par][:, :, HALF:SEQ], in_=SHK[par][bass.ds(oth, 1)].rearrange("o g p t -> (o g) p t")).then_inc(fetsem, 16)
                    e.dma_start(out=VD[par][HALF:SEQ, :], in_=SHV[par][bass.ds(oth, 1)].rearrange("o t c -> (o t) c")).then_inc(fetsem, 16)
                return None
            S.custom("sp", fn, waits)
            def fnp(e, par=par):
                if par is not None:
                    pid = S.pid[id(e)]
                    oth = (pid + 1) % NCORES
                    e.dma_start(out=UD[par][HALF:SEQ, :], in_=SHU[par][bass.ds(oth, 1)].rearrange("o t c -> (o t) c")).then_inc(fetsem, 16)
                    e.dma_start(out=PD[par][:, :, HALF:SEQ], in_=SHP[par][bass.ds(oth, 1)].rearrange("o g p t -> (o g) p t")).then_inc(fetsem, 16)
                return None
            S.custom("pool", fnp, [(xsem, 16 * (k + 1))])
            if par is not None:
                nfet[0] += 4
                S.custom("sp", None, [(fetsem, 16 * nfet[0])])
                S.custom("pool", None, [(fetsem, 16 * nfet[0])])

        xbarrier()
        for c_ in range(NCORES):
            S.dma("sp", modt[:, :].rearrange("p (l h x) -> p l h x", l=L, h=NCORES)[:, :, c_, :],
                  MODS[c_].rearrange("p (l x) -> p l x", l=L), mod_r, writes=[mod_r])
        def MOD(l, i, c, m_):
            o = ((l * 9 + i) * KC + c) * 2 + m_
            return modt[:, o:o + 1]
        for l in range(L):
            for sub in range(3):
                for m_ in range(2):
                    src = modt[:, (l * 9 + 3 * sub + 1) * 32:(l * 9 + 3 * sub + 2) * 32].rearrange("p (c m) -> p c m", m=2)[:, :, m_]
                    dst = amul[:, (l * 3 + sub) * 32:(l * 3 + sub + 1) * 32].rearrange("p (c m) -> p c m", m=2)[:, :, m_]
                    nw = normw[:, (l * 3 + sub) * KC:(l * 3 + sub + 1) * KC]
                    stt(dst, src, 1.0, nw, ALU.add, ALU.mult, [mod_r, normw_r], [amul_r])
            for s_ in range(2):
                src = modt[:, (l * 9 + 6 * s_ + 2) * 32:(l * 9 + 6 * s_ + 3) * 32]
                dst = hgt[:, (l * 2 + s_) * 32:(l * 2 + s_ + 1) * 32]
                ts("dve", dst, src, 0.5, None, ALU.mult, None, [mod_r], [hg_r])
        def AMUL(l, sub, c, m_):
            o = ((l * 3 + sub) * KC + c) * 2 + m_
            return amul[:, o:o + 1]
        def HG(l, s_, c, m_):
            o = ((l * 2 + s_) * KC + c) * 2 + m_
            return hgt[:, o:o + 1]

        def rms_rstd(n, nchunks_src, src_chunk, src_res, dim, sq_view, sq_res):
            pst = PS[6]
            for c in range(nchunks_src):
                if nchunks_src > 1 and c % 3 == 2:
                    tt("pool", sq_view(c), src_chunk(c), src_chunk(c), ALU.mult, src_res, [sq_res])
                else:
                    act_op(sq_view(c), src_chunk(c), AF.Square, src_res, [sq_res])
            for c in range(nchunks_src):
                mm((pst[0][:, 0:n], pst[1]), ones_ap, sq_view(c), c == 0, c == nchunks_src - 1,
                   [sq_res, small_r], mark=(c == nchunks_src - 1))
            act_op(tmpa[:, 0:n], pst[0][:, 0:n], AF.Sqrt, [pst[1]], [tmpa_r], bias=eps_ap, scale=1.0 / dim)
            recip(rstd[:, 0:n], tmpa[:, 0:n], [tmpa_r], [rstd_r])

        epst, eps_r = sb("epst", [128, 1], F32)
        S.op("dve", lambda e: e.memset(epst[:, :], EPS), writes=[eps_r])
        eps_ap = epst[:, 0:1]

        def modulate(l, sub, n, m_):
            sqv = lambda c: sqb[:, c * NT:c * NT + n]
            rms_rstd(n, KC, lambda c: xg[:, c, 0:n], [xg_r], float(D), sqv, sq_r)
            dtmp = [(tmpb, tmpb_r), (tmpc, tmpc_r), (silu_t, silu_r)]
            di = 0
            for c in range(KC):
                if c % 3 == 2:
                    eng, (tm, tm_r) = "pool", (stf, stf_r)
                else:
                    eng, (tm, tm_r) = "dve", dtmp[di % 3]
                    di += 1
                tt(eng, tm[:, 0:n], xg[:, c, 0:n], rstd[:, 0:n], ALU.mult, [xg_r, rstd_r], [tm_r])
                act_op(hb[:, c, 0:n], tm[:, 0:n], AF.Identity, [tm_r, amul_r, mod_r], [hb_r],
                       bias=MOD(l, 3 * sub, c, m_), scale=AMUL(l, sub, c, m_))

        def ffn(l, s_, n, m_):
            sub = 0 if s_ == 0 else 2
            modulate(l, sub, n, m_)
            for j in range(NJ):
                wt, wres = load_w(l, "gu%d" % s_, j)
                pg = PS[(j % 2) * 2]
                pu = PS[(j % 2) * 2 + 1]
                for a, pp in ((0, pg), (1, pu)):
                    for kc in range(KC):
                        mm((pp[0][:, 0:n], pp[1]), wt[:, (a * KC + kc) * 128:(a * KC + kc + 1) * 128], hb[:, kc, 0:n],
                           kc == 0, kc == KC - 1, [wres, hb_r], mark=(kc == KC - 1))
                act_op(silu_t[:, 0:n], pg[0][:, 0:n], AF.Silu, [pg[1]], [silu_r])
                tt("dve", actb[:, j * NT:j * NT + n], silu_t[:, 0:n], pu[0][:, 0:n], ALU.mult, [silu_r, pu[1]], [act_r])
            for c in range(KC):
                wt, wres = load_w(l, "dn%d" % s_, c)
                po = PS[4 + (c % 2)]
                for j in range(NJ):
                    mm((po[0][:, 0:n], po[1]), wt[:, j * 128:(j + 1) * 128], actb[:, j * NT:j * NT + n],
                       j == 0, j == NJ - 1, [wres, act_r], mark=(j == NJ - 1))
                stt(xg[:, c, 0:n], po[0][:, 0:n], HG(l, s_, c, m_), xg[:, c, 0:n], ALU.mult, ALU.add,
                    [po[1], hg_r, xg_r], [xg_r])

        def qk_stage2(l, hi, n, praw):
            gidx = 0 if hi < 8 else 1
            act_op(stg_sq[0][:, 0:n], praw[0][:, 0:n], AF.Square, [praw[1]], [stg_sq[1]])
            pss = PS[6]
            mm((pss[0][:, 0:n], pss[1]), ones_ap, stg_sq[0][:, 0:n], True, True, [stg_sq[1], small_r])
            act_op(tmpa[:, 0:n], pss[0][:, 0:n], AF.Sqrt, [pss[1]], [tmpa_r], bias=eps_ap, scale=1.0 / 128.0)
            recip(rstd[:, 0:n], tmpa[:, 0:n], [tmpa_r], [rstd_r])
            g_ap = qkg[:, l * 2 + gidx:l * 2 + gidx + 1]
            stt(qn_t[0][:, 0:n], praw[0][:, 0:n], g_ap, rstd[:, 0:n], ALU.mult, ALU.mult,
                [praw[1], qkg_r, rstd_r], [qn_t[1]])

        def qk_stage3(n, is_ctx, dst_ap):
            if is_ctx:
                def prod(ap, r):
                    S.op("dve", lambda e: e.tensor_copy(out=ap, in_=qn_t[0][:, 0:n]), reads=[qn_t[1]], writes=[r])
                store_bf(dst_ap, n, prod)
            else:
                prt = PS[7]
                mm((prt[0][:, 0:n], prt[1]), perm_ap, qn_t[0][:, 0:n], True, True, [qn_t[1], small_r])
                tt("dve", tmpb[:, 0:n], qn_t[0][:, 0:n], rope[:, 0:n], ALU.mult, [qn_t[1], rope_r], [tmpb_r])
                tt("dve", tmpc[:, 0:n], prt[0][:, 0:n], rope[:, NT:NT + n], ALU.mult, [prt[1], rope_r], [tmpc_r])
                def prod(ap, r):
                    tt("dve", ap, tmpb[:, 0:n], tmpc[:, 0:n], ALU.add, [tmpb_r, tmpc_r], [r])
                store_bf(dst_ap, n, prod)

        stg_sq = sb("stg_sq", [128, NT], BF16)
        qn_t = sb("qn_t", [128, NT], BF16)
        uft, uft_r = sb("uft", [128, 4 * NT], BF16)

        def stage_A(l, t0, n, is_ctx, after_mod=None):
            m_ = 1 if is_ctx else 0
            par = l % 2
            ffn(l, 0, n, m_)
            modulate(l, 1, n, m_)
            if after_mod is not None:
                after_mod()
            if not is_ctx:
                S.dma("sp", rope[:, :].rearrange("p (a t) -> p a t", a=2)[:, :, 0:n], IN["rope"][:, :, t0:t0 + n], rope_r, writes=[rope_r])
            kv_only = is_ctx and l == L - 1
            heads = list(range(8 if kv_only else 0, 10))
            def qk_dst(hi):
                return QD[par][hi, :, t0:t0 + n] if hi < 8 else KD[par][hi - 8, :, lsl(t0, n, is_ctx)]
            for st in range(len(heads) + 2):
                if st < len(heads):
                    hi = heads[st]
                    wt, wres = load_w(l, "wqk", hi)
                    pp = PS[hi % 4]
                    for kc in range(KC):
                        mm((pp[0][:, 0:n], pp[1]), wt[:, kc * 128:(kc + 1) * 128], hb[:, kc, 0:n], kc == 0, kc == KC - 1,
                           [wres, hb_r], mark=(kc == KC - 1))
                if 0 <= st - 2 < len(heads):
                    qk_stage3(n, is_ctx, qk_dst(heads[st - 2]))
                if 0 <= st - 1 < len(heads):
                    hi1 = heads[st - 1]
                    qk_stage2(l, hi1, n, PS[hi1 % 4])
            wt, wres = load_w(l, "wv", 0)
            for tb in range(n // 128):
                pp = PS[4 + (tb % 2)]
                for kc in range(KC):
                    mm((pp[0][:, 0:256], pp[1]), hb[:, kc, tb * 128:(tb + 1) * 128], wt[:, kc * 256:(kc + 1) * 256],
                       kc == 0, kc == KC - 1, [wres, hb_r], mark=(kc == KC - 1))
                def prod(ap, r, pp=pp):
                    act_op(ap, pp[0][:, 0:256], AF.Copy, [pp[1]], [r])
                store_bf(VD[par][lsl(t0 + tb * 128, 128, is_ctx), :], 256, prod)
            if kv_only:
                return
            for ci in range(8):
                wt, wres = load_w(l, "wfp", ci)
                pp = PS[ci % 4]
                for kc in range(KC):
                    mm((pp[0][:, 0:n], pp[1]), wt[:, kc * 128:(kc + 1) * 128], hb[:, kc, 0:n], kc == 0, kc == KC - 1,
                       [wres, hb_r], mark=(kc == KC - 1))
                if ci < 4:
                    act_op(uft[:, ci * NT:ci * NT + n], pp[0][:, 0:n], AF.Copy, [pp[1]], [uft_r])
                else:
                    S.op("dve", lambda e, pp=pp: e.tensor_copy(out=stf[:, 0:n], in_=pp[0][:, 0:n]), reads=[pp[1]], writes=[stf_r])
                    S.dma("sp", PD[par][ci - 4, :, lsl(t0, n, is_ctx)], stf[:, 0:n], stf_r,
                          reads=[stf_r], store=True)
            for tb in range(n // 128):
                for gi in range(4):
                    pp = PS[4 + ((tb * 4 + gi) % 2)]
                    mm((pp[0][:, 0:256], pp[1]), uft[:, gi * NT + tb * 128:gi * NT + (tb + 1) * 128], csc[:, :], True, True,
                       [uft_r, csc_r])
                    def prod(ap, r, pp=pp):
                        act_op(ap, pp[0][:, 0:256], AF.Copy, [pp[1]], [r])
                    store_bf(UD[par][lsl(t0 + tb * 128, 128, is_ctx), gi * 256:(gi + 1) * 256],
                             256, prod)
            for ci in range(48):
                wt, wres = load_w(l, "wg", ci)
                pp = PS[ci % 4]
                for kc in range(KC):
                    mm((pp[0][:, 0:n], pp[1]), wt[:, kc * 128:(kc + 1) * 128], hb[:, kc, 0:n], kc == 0, kc == KC - 1,
                       [wres, hb_r], mark=(kc == KC - 1))
                def prod(ap, r, pp=pp):
                    act_op(ap, pp[0][:, 0:n], AF.Sigmoid, [pp[1]], [r])
                store_bf(GD[par][ci, :, t0:t0 + n], n, prod)

        def attention(l, t0, n, is_ctx):
            par = l % 2
            S.dma("sp", qgt.rearrange("p (h t) -> p h t", h=8)[:, :, 0:n], QD[par][:, :, t0:t0 + n].rearrange("h p t -> p h t"),
                  qgt_r, writes=[qgt_r])
            ktv = kt.rearrange("p (g t) -> p g t", g=2)
            vtv = vt.rearrange("p (b c) -> p b c", b=8)
            S.dma("sp", ktv[:, :, 768:1024], KD[par][:, :, SEQ:SEQ + CTX].rearrange("g p t -> p g t"), kt_r, writes=[kt_r])
            S.dma("sp", vtv[:, 6:8, :], VD[par][SEQ:SEQ + CTX, :].rearrange("(b k) c -> k b c", b=2), vt_r, writes=[vt_r])
            if not is_ctx:
                if t0 == 0:
                    S.dma("sp", ktv[:, :, 0:128], KD[par][:, :, SEQ - 128:SEQ].rearrange("g p t -> p g t"), kt_r, writes=[kt_r])
                    S.dma("sp", vtv[:, 0:1, :], VD[par][SEQ - 128:SEQ, :].rearrange("(b k) c -> k b c", k=128), vt_r, writes=[vt_r])
                    S.dma("sp", ktv[:, :, 128:768], KD[par][:, :, 0:640].rearrange("g p t -> p g t"), kt_r, writes=[kt_r])
                    S.dma("sp", vtv[:, 1:6, :], VD[par][0:640, :].rearrange("(b k) c -> k b c", k=128), vt_r, writes=[vt_r])
                else:
                    S.dma("sp", ktv[:, :, 0:768], KD[par][:, :, t0 - 128:t0 + 640].rearrange("g p t -> p g t"), kt_r, writes=[kt_r])
                    S.dma("sp", vtv[:, 0:6, :], VD[par][t0 - 128:t0 + 640, :].rearrange("(b k) c -> k b c", k=128), vt_r, writes=[vt_r])
            qv = qgt.rearrange("p (h t) -> p h t", h=8)
            atv = at.rearrange("p (h t) -> p h t", h=8)
            it = 0
            for qb in range(n // 128):
                for g2 in range(2):
                    blocks = []
                    if not is_ctx:
                        first_q = (t0 == 0 and qb == 0)
                        last_q = (t0 + n == HALF and qb == n // 128 - 1)
                        mL = (emask, emask_r, 0) if first_q else (masks, masks_r, 0)
                        mR = (emask, emask_r, 1) if last_q else (masks, masks_r, 1)
                        blocks = [(qb, mL), (qb + 1, None), (qb + 2, mR)]
                        if False:
                            blocks = blocks[1:]
                    blocks += [(6, None), (7, None)]
                    po = PS[2 + 2 * (it % 2)]
                    pz = PS[3 + 2 * (it % 2)]
                    rhs_q = qv[:, 4 * g2:4 * g2 + 4, qb * 128:(qb + 1) * 128]
                    def qk_(bi):
                        slot, mi = blocks[bi]
                        pst = PS[bi % 2]
                        mm((pst[0][:, :].rearrange("p (h t) -> p h t", h=4), pst[1]), ktv[:, g2, slot * 128:(slot + 1) * 128], rhs_q,
                           True, mi is None, [kt_r, qgt_r])
                        if mi is not None:
                            mm((pst[0][:, :], pst[1]), ident_ap, mi[0][:, mi[2] * 512:(mi[2] + 1) * 512], False, True, [small_r, mi[1]])
                        eb, eb_r = ebuf[bi % 2]
                        act_op(eb, pst[0][:, :], AF.Exp, [pst[1]], [eb_r], scale=att_scale)
                    def pv_(bi):
                        slot, mi = blocks[bi]
                        eb, eb_r = ebuf[bi % 2]
                        first = bi == 0
                        last = bi == len(blocks) - 1
                        mm((po[0][:, :], po[1]), vtv[:, slot, g2 * 128:(g2 + 1) * 128], eb, first, last, [vt_r, eb_r], mark=True)
                        mm((pz[0][:, :], pz[1]), ones_ap, eb, first, last, [small_r, eb_r], mark=True)
                    for bi in range(len(blocks) + 1):
                        if bi < len(blocks):
                            qk_(bi)
                        if bi >= 1:
                            pv_(bi - 1)
                    for h4 in range(4):
                        hh = 4 * g2 + h4
                        ts("dve", rden[:, h4 * 128:(h4 + 1) * 128], pz[0][:, h4 * 128:(h4 + 1) * 128],
                           sexp[:, l * 8 + hh:l * 8 + hh + 1], None, ALU.add, None, [pz[1], sexp_r], [rden_r])
                    recip(rden[:, :], rden[:, :], [rden_r], [rden_r])
                    tt("dve", atv[:, 4 * g2:4 * g2 + 4, qb * 128:(qb + 1) * 128], po[0][:, :].rearrange("p (h t) -> p h t", h=4),
                       rden[:, :].rearrange("p (h t) -> p h t", h=4), ALU.mult, [po[1], rden_r], [at_r])
                    it += 1

        def fourier(l, t0, n, is_ctx):
            par = l % 2
            if is_ctx:
                nsc, sbase, rstride, cbase = CTX // 128, SEQ, SEQ // CTX, 0
            else:
                nsc, sbase, rstride, cbase = SEQ // 128, 0, 1, None
            k = 0
            for sc in range(nsc):
                ut, ur = ucs[k % 3]
                tb_, tr = tabt[k % 3]
                k += 1
                S.dma("sp", ut, UD[par][sbase + sc * 128:sbase + (sc + 1) * 128, :], ur, writes=[ur])
                tv = tb_.rearrange("p (a t) -> p a t", a=2)
                if rstride == 1:
                    src = IN["tab"][:, sc * 128:(sc + 1) * 128, t0:t0 + n].rearrange("a s t -> s a t")
                else:
                    src = IN["ctab"][:, sc * 128:(sc + 1) * 128, 0:n].rearrange("a s t -> s a t")
                S.dma("sp", tv[:, :, 0:n], src, tr, writes=[tr])
                for c4 in range(4):
                    for cs in range(2):
                        first = sc == 0 and cs == 0
                        last = sc == nsc - 1 and cs == 1
                        mm((PS[c4][0][:, 0:n], PS[c4][1]), ut[:, (c4 * 2 + cs) * 128:(c4 * 2 + cs + 1) * 128], tv[:, cs, 0:n],
                           first, last, [ur, tr], mark=(last or (c4 == 3 and cs == 1)))
            sc_f = float(np.sqrt(SEQ / CTX)) if is_ctx else 1.0
            for c4 in range(4):
                act_op(yt[:, c4 * NT:c4 * NT + n], PS[c4][0][:, 0:n], AF.Copy, [PS[c4][1]], [yt_r], scale=sc_f)

        def pool_mix(l, t0, n, is_ctx):
            par = l % 2
            W = n + 16
            uv = upt.rearrange("p (g t) -> p g t", g=4)
            if is_ctx:
                S.op("dve", lambda e: e.memset(upt[:, :], 0.0), writes=[upt_r])
                S.dma("sp", uv[:, :, 8:8 + n], PD[par][:, :, SEQ:SEQ + CTX].rearrange("g p t -> p g t"), upt_r, writes=[upt_r])
            elif t0 == 0:
                S.dma("sp", uv[:, :, 0:8], PD[par][:, :, SEQ - 8:SEQ].rearrange("g p t -> p g t"), upt_r, writes=[upt_r])
                S.dma("sp", uv[:, :, 8:W], PD[par][:, :, 0:n + 8].rearrange("g p t -> p g t"), upt_r, writes=[upt_r])
            else:
                S.dma("sp", uv[:, :, 0:W], PD[par][:, :, t0 - 8:t0 + n + 8].rearrange("g p t -> p g t"), upt_r, writes=[upt_r])
            if not is_ctx and t0 == 0:
                for gq in range(4):
                    ts("dve", uv[:, gq, 0:8], uv[:, gq, 0:8], eflag[:, 0:1], None, ALU.mult, None, [upt_r, eflag_r], [upt_r])
            if not is_ctx and t0 + n == HALF:
                for gq in range(4):
                    ts("dve", uv[:, gq, 8 + n:W], uv[:, gq, 8 + n:W], eflag[:, 1:2], None, ALU.mult, None, [upt_r, eflag_r], [upt_r])
            rsl = lsl(t0, n, is_ctx)
            S.dma("sp", rct.rearrange("p (g t) -> p g t", g=4)[:, :, 0:n],
                  IN["rc"][:, :, (HALF if is_ctx else t0):(HALF if is_ctx else t0) + n], rct_r, writes=[rct_r])
            rv = rct.rearrange("p (g t) -> p g t", g=4)
            for gi, w in enumerate((2, 4, 8, 16)):
                u = uv[:, gi, :]
                cur, cur_r, ln_ = u, upt_r, W
                step = 1
                bufs = [(pa, pa_r), (pb, pb_r)]
                bi = 0
                while step < w:
                    o, o_r = bufs[bi % 2]
                    bi += 1
                    nl = ln_ - step
                    tt("dve", o[:, 0:nl], cur[:, 0:nl], cur[:, step:step + nl], ALU.add, [cur_r], [o_r])
                    cur, cur_r, ln_ = o, o_r, nl
                    step *= 2
                st_ = 8 - w // 2
                o, o_r = bufs[bi % 2]
                tt("dve", o[:, 0:n], cur[:, st_:st_ + n], rv[:, gi, 0:n], ALU.mult, [cur_r, rct_r], [o_r])
                tt("dve", pt[:, gi * NT:gi * NT + n], o[:, 0:n], u[:, 8:8 + n], ALU.subtract, [o_r, upt_r], [pt_r])

        def merge_out(l, t0, n, is_ctx):
            par = l % 2
            m_ = 1 if is_ctx else 0
            atv = at.rearrange("p (h t) -> p h t", h=8)
            for c in range(KC):
                wt, wres = load_w(l, "wmg", c)
                gt, gr = gat[c % 2]
                gv = gt.rearrange("p (a t) -> p a t", a=3)
                S.dma("sp", gv[:, :, 0:n], GD[par][:, :, t0:t0 + n].rearrange("(a c) p t -> c p a t", a=3)[c], gr, writes=[gr])
                o3 = 0 if c % 2 == 0 else 4
                pA, pF, pP = PS[o3], PS[o3 + 1], PS[o3 + 2]
                for kc in range(8):
                    mm((pA[0][:, 0:n], pA[1]), wt[:, kc * 128:(kc + 1) * 128], atv[:, kc, 0:n], kc == 0, kc == 7, [wres, at_r], mark=(kc == 7))
                for kc in range(4):
                    mm((pF[0][:, 0:n], pF[1]), wt[:, (8 + kc) * 128:(9 + kc) * 128], yt[:, kc * NT:kc * NT + n], kc == 0, kc == 3,
                       [wres, yt_r], mark=(kc == 3))
                gi = c // 4
                mm((pP[0][:, 0:n], pP[1]), wt[:, 12 * 128:13 * 128], pt[:, gi * NT:gi * NT + n], True, True, [wres, pt_r])
                tt("dve", tmpa[:, 0:n], pA[0][:, 0:n], gv[:, 0, 0:n], ALU.mult, [pA[1], gr], [tmpa_r])
                tt("dve", tmpb[:, 0:n], pF[0][:, 0:n], gv[:, 1, 0:n], ALU.mult, [pF[1], gr], [tmpb_r])
                stt(tmpc[:, 0:n], pP[0][:, 0:n], pscale[:, l * KC + c:l * KC + c + 1], gv[:, 2, 0:n], ALU.mult, ALU.mult,
                    [pP[1], pscale_r, gr], [tmpc_r])
                tt("dve", tmpa[:, 0:n], tmpa[:, 0:n], tmpb[:, 0:n], ALU.add, [tmpa_r, tmpb_r], [tmpa_r])
                tt("dve", hb[:, c, 0:n], tmpa[:, 0:n], tmpc[:, 0:n], ALU.add, [tmpa_r, tmpc_r], [hb_r])
            for c in range(KC):
                wt, wres = load_w(l, "wout", c)
                po = PS[3 if c % 2 == 0 else 7]
                for kc in range(KC):
                    mm((po[0][:, 0:n], po[1]), wt[:, kc * 128:(kc + 1) * 128], hb[:, kc, 0:n], kc == 0, kc == KC - 1,
                       [wres, hb_r], mark=(kc == KC - 1))
                stt(xg[:, c, 0:n], po[0][:, 0:n], MOD(l, 5, c, m_), xg[:, c, 0:n], ALU.mult, ALU.add, [po[1], mod_r, xg_r], [xg_r])

        out_toks = []
        for sp_ in range(L + 1):
            if 1 <= sp_ + 1 < L:
                for i_ in range(2):
                    precast(sp_ + 1, i_)
            act_groups = [g_ for g_ in groups if (sp_ >= 1 and not (g_[2] and sp_ - 1 == L - 1)) or sp_ < L]
            preloaded = [False]
            for gi_, (t0, n, is_ctx) in enumerate(act_groups):
                do_bc = sp_ >= 1 and not (is_ctx and sp_ - 1 == L - 1)
                do_a = sp_ < L
                src = IN["xin"] if sp_ == 0 else XD
                if not preloaded[0]:
                    S.dma("sp", xg[:, :, 0:n], src[:, :, t0:t0 + n].rearrange("k p t -> p k t"), xg_r, writes=[xg_r])
                preloaded[0] = False
                def after_mod(gi_=gi_, t0=t0, n=n, src=src):
                    S.dma("sp", XD[:, :, t0:t0 + n].rearrange("k p t -> p k t"), xg[:, :, 0:n], xg_r, reads=[xg_r], store=True)
                    if gi_ + 1 < len(act_groups):
                        t1, n1, _ = act_groups[gi_ + 1]
                        S.dma("sp", xg[:, :, 0:n1], src[:, :, t1:t1 + n1].rearrange("k p t -> p k t"), xg_r, writes=[xg_r])
                        preloaded[0] = True
                if do_bc:
                    lb = sp_ - 1
                    attention(lb, t0, n, is_ctx)
                    fourier(lb, t0, n, is_ctx)
                    pool_mix(lb, t0, n, is_ctx)
                    merge_out(lb, t0, n, is_ctx)
                    ffn(lb, 1, n, 1 if is_ctx else 0)
                if do_a:
                    stage_A(sp_, t0, n, is_ctx, after_mod)
                if sp_ == L:
                    tok = S.dma("sp", yout[:, :, t0:t0 + n].rearrange("k p t -> p k t"), xg[:, :, 0:n], xg_r, reads=[xg_r], store=True)
                    out_toks.append(tok)
            while pc_queue:
                precast(*pc_queue.pop(0))
            if sp_ < L:
                xbarrier(par=sp_ % 2)
        S.final_wait("sp", out_toks[-1:])

        with nc.Block() as block:
            S.emit(block)
    return nc


_SKIP = ("rope", "small", "masks", "csc", "tab", "rc")


def run(inputs, cfg):
    maps = _prep_inputs(inputs, cfg)
    shapes = {k: (v.shape, v.dtype) for k, v in maps[0].items()}
    nc = build_program(cfg, shapes)
    res = run_bass_kernel_spmd(nc, maps, core_ids=list(range(NCORES)))
    yT = np.concatenate([res.results[c]["yout"] for c in range(NCORES)], 2)
    SEQ = cfg["SEQ"]
    return np.ascontiguousarray(yT.reshape(D, SEQ).T.reshape(1, SEQ, D)).astype(np.float32)


def kernel(**inputs):
    return run(inputs, CFG)
```
